# Optimizing a Trainium2 kernel written in Bass

```python
import math
import jax
import jax.numpy as jnp
from jax import lax
import numpy as np


D_MODEL = 1024
BATCH = 8
SEQ = 2048
DEPTH = 2

N_EVEN = (DEPTH + 1) // 2
N_ODD = DEPTH // 2
MIX_WIDTH = D_MODEL
Q_BLOCK = 128
RMS_EPS = 1e-6
F32 = jnp.float32

MLA_HEADS = 8
MLA_NOPE = 64
MLA_ROPE = 32
MLA_V = 64
MLA_Q_LORA = 384
MLA_KV_LORA = 256
ROPE_BASE = 10000.0

S5_WIDTH = MIX_WIDTH - MLA_HEADS * MLA_V
S5_GROUP = 16
S5_GROUPS = S5_WIDTH // S5_GROUP
S5_STATE = 64
S5_DT_MIN = 0.001
S5_DT_MAX = 0.1

SB_HEADS = 8
SB_HEAD_DIM = 64
SB_WIDTH = SB_HEADS * SB_HEAD_DIM

LRU_WIDTH = MIX_WIDTH - SB_WIDTH
LRU_BLOCKS = 8
LRU_BLOCK_DIM = LRU_WIDTH // LRU_BLOCKS
LRU_C = 8.0
CONV_WIDTH = 4

FFN_HIDDEN = -(-8 * D_MODEL // (3 * 256)) * 256

IN_EVEN = MLA_Q_LORA + MLA_KV_LORA + MLA_ROPE + S5_WIDTH
IN_ODD = 3 * SB_WIDTH + 2 * LRU_WIDTH

kernel_name = 'hybrid_mla_s5_stickbreak_rglru_block'


def _rmsnorm(x, g):
    x32 = x.astype(F32)
    y = x32 * lax.rsqrt(jnp.mean(x32 * x32, axis=-1, keepdims=True) + RMS_EPS)
    return (y * g.astype(F32)).astype(x.dtype)


def _rope(x, pos):
    half = x.shape[-1] // 2
    inv = ROPE_BASE ** (-jnp.arange(half, dtype=F32) / half)
    ang = pos.astype(F32)[..., None, None] * inv
    cos, sin = jnp.cos(ang), jnp.sin(ang)
    x1, x2 = x[..., :half].astype(F32), x[..., half:].astype(F32)
    return jnp.concatenate([x1 * cos - x2 * sin, x2 * cos + x1 * sin], axis=-1).astype(x.dtype)


def _linear_scan(a, b):
    def op(left, right):
        return (left[0] * right[0], right[0] * left[1] + right[1])
    return lax.associative_scan(op, (a, b), axis=1)[1]


def _split_query_blocks(q):
    b, h, l, d = q.shape
    return q.reshape(b, h, l // Q_BLOCK, Q_BLOCK, d).transpose(2, 0, 1, 3, 4)


def _merge_query_blocks(o):
    nb, b, h, qb, d = o.shape
    return o.transpose(1, 2, 0, 3, 4).reshape(b, h, nb * qb, d)


def _causal_softmax_attn(q, k, v):
    seq = q.shape[2]
    scale = q.shape[-1] ** -0.5
    kpos = jnp.arange(seq)

    def block(args):
        qi, i = args
        s = jnp.einsum('bhqd,bhkd->bhqk', qi, k).astype(F32) * scale
        qpos = i * Q_BLOCK + jnp.arange(Q_BLOCK)
        s = jnp.where(kpos[None, :] <= qpos[:, None], s, -1e30)
        p = jax.nn.softmax(s, axis=-1)
        return jnp.einsum('bhqk,bhkd->bhqd', p.astype(v.dtype), v)

    o = lax.map(block, (_split_query_blocks(q), jnp.arange(seq // Q_BLOCK)))
    return _merge_query_blocks(o)


def _stick_breaking_attn(q, k, v):
    seq = q.shape[2]
    scale = q.shape[-1] ** -0.5
    kpos = jnp.arange(seq)

    def block(args):
        qi, i = args
        z = jnp.einsum('bhqd,bhkd->bhqk', qi, k).astype(F32) * scale
        qpos = i * Q_BLOCK + jnp.arange(Q_BLOCK)
        mask = kpos[None, :] < qpos[:, None]
        log_beta = jax.nn.log_sigmoid(z)
        log_1m_beta = jnp.where(mask, jax.nn.log_sigmoid(-z), 0.0)
        suffix = lax.cumsum(log_1m_beta, axis=3, reverse=True) - log_1m_beta
        w = jnp.where(mask, jnp.exp(log_beta + suffix), 0.0)
        return jnp.einsum('bhqk,bhkd->bhqd', w.astype(v.dtype), v)

    o = lax.map(block, (_split_query_blocks(q), jnp.arange(seq // Q_BLOCK)))
    return _merge_query_blocks(o)


def _s5(u, lam_re, lam_im, log_dt, b_re, b_im, c_re, c_im, d_skip, w_glu, b_glu):
    bsz, seq, _ = u.shape
    ug = u.astype(F32).reshape(bsz, seq, S5_GROUPS, S5_GROUP)
    lam = lax.complex(jnp.minimum(lam_re.astype(F32), -1e-4), lam_im.astype(F32))
    dt = jnp.exp(log_dt.astype(F32))[:, None]
    lam_bar = jnp.exp(lam * dt)
    b_mat = lax.complex(b_re.astype(F32), b_im.astype(F32))
    b_bar = ((lam_bar - 1.0) / lam)[..., None] * b_mat
    bu = jnp.einsum('blgc,gpc->blgp', ug.astype(jnp.complex64), b_bar)
    states = _linear_scan(jnp.broadcast_to(lam_bar, bu.shape), bu)
    c_mat = lax.complex(c_re.astype(F32), c_im.astype(F32))
    y = jnp.real(jnp.einsum('blgp,gcp->blgc', states, c_mat)) + d_skip.astype(F32) * ug
    y = jax.nn.gelu(y.reshape(bsz, seq, S5_WIDTH))
    y = y * jax.nn.sigmoid(y @ w_glu.astype(F32) + b_glu.astype(F32))
    return y.astype(u.dtype)


def _even_mixer(u, pos, w_in, q_norm_g, kv_norm_g, w_q_up, w_kv_up,
                lam_re, lam_im, log_dt, b_re, b_im, c_re, c_im, d_skip, w_glu, b_glu):
    bsz, seq, _ = u.shape
    proj = u @ w_in
    q_lat, kv_lat, k_rope, s5_u = jnp.split(
        proj, [MLA_Q_LORA, MLA_Q_LORA + MLA_KV_LORA, MLA_Q_LORA + MLA_KV_LORA + MLA_ROPE], axis=-1)
    q = (_rmsnorm(q_lat, q_norm_g) @ w_q_up).reshape(bsz, seq, MLA_HEADS, MLA_NOPE + MLA_ROPE)
    q = jnp.concatenate([q[..., :MLA_NOPE], _rope(q[..., MLA_NOPE:], pos)], axis=-1)
    kv = (_rmsnorm(kv_lat, kv_norm_g) @ w_kv_up).reshape(bsz, seq, MLA_HEADS, MLA_NOPE + MLA_V)
    k_r = _rope(k_rope[:, :, None, :], pos)
    k = jnp.concatenate([kv[..., :MLA_NOPE], jnp.broadcast_to(k_r, (bsz, seq, MLA_HEADS, MLA_ROPE))], axis=-1)
    v = kv[..., MLA_NOPE:]
    attn = _causal_softmax_attn(q.transpose(0, 2, 1, 3), k.transpose(0, 2, 1, 3), v.transpose(0, 2, 1, 3))
    attn = attn.transpose(0, 2, 1, 3).reshape(bsz, seq, MLA_HEADS * MLA_V)
    ssm = _s5(s5_u, lam_re, lam_im, log_dt, b_re, b_im, c_re, c_im, d_skip, w_glu, b_glu)
    return jnp.concatenate([attn, ssm], axis=-1)


def _odd_mixer(u, w_in, conv_w, conv_b, w_a, b_a, w_x, b_x, lam):
    bsz, seq, _ = u.shape
    proj = u @ w_in
    q, k, v, xr, yg = jnp.split(
        proj, [SB_WIDTH, 2 * SB_WIDTH, 3 * SB_WIDTH, 3 * SB_WIDTH + LRU_WIDTH], axis=-1)

    def heads(t):
        return t.reshape(bsz, seq, SB_HEADS, SB_HEAD_DIM).transpose(0, 2, 1, 3)

    sb = _stick_breaking_attn(heads(q), heads(k), heads(v))
    sb = sb.transpose(0, 2, 1, 3).reshape(bsz, seq, SB_WIDTH)
    xc = lax.conv_general_dilated(
        xr, conv_w[:, None, :].astype(xr.dtype), window_strides=(1,), padding=[(CONV_WIDTH - 1, 0)],
        dimension_numbers=('NWC', 'WIO', 'NWC'), feature_group_count=LRU_WIDTH) + conv_b
    xg = xc.reshape(bsz, seq, LRU_BLOCKS, LRU_BLOCK_DIM)
    r = jax.nn.sigmoid(jnp.einsum('blnc,ncd->blnd', xg, w_a).reshape(bsz, seq, LRU_WIDTH) + b_a)
    i = jax.nn.sigmoid(jnp.einsum('blnc,ncd->blnd', xg, w_x).reshape(bsz, seq, LRU_WIDTH) + b_x)
    log_a = LRU_C * r.astype(F32) * jax.nn.log_sigmoid(lam.astype(F32))
    a = jnp.exp(log_a)
    inp = jnp.sqrt(-jnp.expm1(2.0 * log_a)) * (i * xc).astype(F32)
    h = _linear_scan(a, inp)
    rec = h.astype(u.dtype) * jax.nn.gelu(yg)
    return jnp.concatenate([sb, rec], axis=-1)


def _swiglu(u, w_gate, w_up, w_down):
    return (jax.nn.silu(u @ w_gate) * (u @ w_up)) @ w_down


def setup_inputs(seed: int = 0) -> dict:
    key = jax.random.key(seed)
    ks = iter(jax.random.split(key, 64))

    def nrm(shape, scale):
        return scale * jax.random.normal(next(ks), shape, F32)

    x = nrm((BATCH, SEQ, D_MODEL), 1.0)
    c = nrm((BATCH, D_MODEL), 1.0)
    positions = (jax.random.randint(next(ks), (BATCH, 1), 0, 1024) + jnp.arange(SEQ)[None, :]).astype(jnp.int32)
    mod_w = nrm((DEPTH, D_MODEL, 6 * D_MODEL), 0.5 * D_MODEL ** -0.5)
    mod_b = nrm((DEPTH, 6 * D_MODEL), 0.01)
    norm_g = 1.0 + nrm((DEPTH, 4, D_MODEL), 0.01)
    w_out = nrm((DEPTH, MIX_WIDTH, D_MODEL), MIX_WIDTH ** -0.5)
    ffn_w_gate = nrm((DEPTH, D_MODEL, FFN_HIDDEN), D_MODEL ** -0.5)
    ffn_w_up = nrm((DEPTH, D_MODEL, FFN_HIDDEN), D_MODEL ** -0.5)
    ffn_w_down = nrm((DEPTH, FFN_HIDDEN, D_MODEL), FFN_HIDDEN ** -0.5)
    even_w_in = nrm((N_EVEN, D_MODEL, IN_EVEN), D_MODEL ** -0.5)
    mla_q_norm_g = 1.0 + nrm((N_EVEN, MLA_Q_LORA), 0.01)
    mla_kv_norm_g = 1.0 + nrm((N_EVEN, MLA_KV_LORA), 0.01)
    mla_w_q_up = nrm((N_EVEN, MLA_Q_LORA, MLA_HEADS * (MLA_NOPE + MLA_ROPE)), MLA_Q_LORA ** -0.5)
    mla_w_kv_up = nrm((N_EVEN, MLA_KV_LORA, MLA_HEADS * (MLA_NOPE + MLA_V)), MLA_KV_LORA ** -0.5)
    s5_lam_re = -0.5 + nrm((N_EVEN, S5_GROUPS, S5_STATE), 0.01)
    s5_lam_im = math.pi * jnp.arange(S5_STATE, dtype=F32) + nrm((N_EVEN, S5_GROUPS, S5_STATE), 0.01)
    s5_log_dt = jax.random.uniform(next(ks), (N_EVEN, S5_GROUPS), F32, math.log(S5_DT_MIN), math.log(S5_DT_MAX))
    s5_b_re = nrm((N_EVEN, S5_GROUPS, S5_STATE, S5_GROUP), (2 * S5_GROUP) ** -0.5)
    s5_b_im = nrm((N_EVEN, S5_GROUPS, S5_STATE, S5_GROUP), (2 * S5_GROUP) ** -0.5)
    s5_c_re = nrm((N_EVEN, S5_GROUPS, S5_GROUP, S5_STATE), S5_STATE ** -0.5)
    s5_c_im = nrm((N_EVEN, S5_GROUPS, S5_GROUP, S5_STATE), S5_STATE ** -0.5)
    s5_d = nrm((N_EVEN, S5_GROUPS, S5_GROUP), 1.0)
    s5_w_glu = nrm((N_EVEN, S5_WIDTH, S5_WIDTH), S5_WIDTH ** -0.5)
    s5_b_glu = nrm((N_EVEN, S5_WIDTH), 0.01)
    odd_w_in = nrm((N_ODD, D_MODEL, IN_ODD), D_MODEL ** -0.5)
    lru_conv_w = nrm((N_ODD, CONV_WIDTH, LRU_WIDTH), CONV_WIDTH ** -0.5)
    lru_conv_b = nrm((N_ODD, LRU_WIDTH), 0.01)
    lru_w_a = nrm((N_ODD, LRU_BLOCKS, LRU_BLOCK_DIM, LRU_BLOCK_DIM), LRU_BLOCK_DIM ** -0.5)
    lru_b_a = nrm((N_ODD, LRU_WIDTH), 0.01)
    lru_w_x = nrm((N_ODD, LRU_BLOCKS, LRU_BLOCK_DIM, LRU_BLOCK_DIM), LRU_BLOCK_DIM ** -0.5)
    lru_b_x = nrm((N_ODD, LRU_WIDTH), 0.01)
    a_c = jax.random.uniform(next(ks), (N_ODD, LRU_WIDTH), F32, 0.9, 0.999)
    a0 = a_c ** (1.0 / LRU_C)
    lru_lambda = jnp.log(a0) - jnp.log1p(-a0)
    return {
        'x': x, 'c': c, 'positions': positions,
        'mod_w': mod_w, 'mod_b': mod_b, 'norm_g': norm_g, 'w_out': w_out,
        'ffn_w_gate': ffn_w_gate, 'ffn_w_up': ffn_w_up, 'ffn_w_down': ffn_w_down,
        'even_w_in': even_w_in, 'mla_q_norm_g': mla_q_norm_g, 'mla_kv_norm_g': mla_kv_norm_g,
        'mla_w_q_up': mla_w_q_up, 'mla_w_kv_up': mla_w_kv_up,
        's5_lam_re': s5_lam_re, 's5_lam_im': s5_lam_im, 's5_log_dt': s5_log_dt,
        's5_b_re': s5_b_re, 's5_b_im': s5_b_im, 's5_c_re': s5_c_re, 's5_c_im': s5_c_im,
        's5_d': s5_d, 's5_w_glu': s5_w_glu, 's5_b_glu': s5_b_glu,
        'odd_w_in': odd_w_in, 'lru_conv_w': lru_conv_w, 'lru_conv_b': lru_conv_b,
        'lru_w_a': lru_w_a, 'lru_b_a': lru_b_a, 'lru_w_x': lru_w_x, 'lru_b_x': lru_b_x,
        'lru_lambda': lru_lambda,
    }


def reference(x, c, positions, mod_w, mod_b, norm_g, w_out, ffn_w_gate, ffn_w_up, ffn_w_down,
              even_w_in, mla_q_norm_g, mla_kv_norm_g, mla_w_q_up, mla_w_kv_up,
              s5_lam_re, s5_lam_im, s5_log_dt, s5_b_re, s5_b_im, s5_c_re, s5_c_im,
              s5_d, s5_w_glu, s5_b_glu,
              odd_w_in, lru_conv_w, lru_conv_b, lru_w_a, lru_b_a, lru_w_x, lru_b_x, lru_lambda):
    h = x
    c_act = jax.nn.silu(c)
    for layer in range(DEPTH):
        mod = (c_act @ mod_w[layer] + mod_b[layer])[:, None, :]
        sh_mix, sc_mix, g_mix, sh_ffn, sc_ffn, g_ffn = jnp.split(mod, 6, axis=-1)
        u = _rmsnorm(h, norm_g[layer, 0]) * (1.0 + sc_mix) + sh_mix
        j = layer // 2
        if layer % 2 == 0:
            m = _even_mixer(u, positions, even_w_in[j], mla_q_norm_g[j], mla_kv_norm_g[j],
                            mla_w_q_up[j], mla_w_kv_up[j], s5_lam_re[j], s5_lam_im[j], s5_log_dt[j],
                            s5_b_re[j], s5_b_im[j], s5_c_re[j], s5_c_im[j], s5_d[j],
                            s5_w_glu[j], s5_b_glu[j])
        else:
            m = _odd_mixer(u, odd_w_in[j], lru_conv_w[j], lru_conv_b[j], lru_w_a[j], lru_b_a[j],
                           lru_w_x[j], lru_b_x[j], lru_lambda[j])
        h = h + g_mix * _rmsnorm(m @ w_out[layer], norm_g[layer, 1])
        u = _rmsnorm(h, norm_g[layer, 2]) * (1.0 + sc_ffn) + sh_ffn
        f = _swiglu(u, ffn_w_gate[layer], ffn_w_up[layer], ffn_w_down[layer])
        h = h + g_ffn * _rmsnorm(f, norm_g[layer, 3])
    return h
```

```python
import math
from contextlib import ExitStack
import numpy as np
import ml_dtypes
import concourse.bass as bass
import concourse.mybir as mybir
from concourse.bass_utils import run_bass_kernel_spmd

F32 = mybir.dt.float32
BF16 = mybir.dt.bfloat16
I32 = mybir.dt.int32
AF = mybir.ActivationFunctionType
ALU = mybir.AluOpType
AX = mybir.AxisListType


class Ev:
    __slots__ = ("key", "sem", "val")

    def __init__(self, key, sem, val):
        self.key, self.sem, self.val = key, sem, val


class Sched:
    ENGS = ("pe", "act", "dve", "pool", "sp")

    def __init__(self, nc, es):
        self.nc = nc
        self.q = {k: [] for k in self.ENGS}
        self.sem = {k: es.enter_context(nc.semaphore("s_" + k)) for k in ("pe", "act", "dve", "pool")}
        self.cnt = {k: 0 for k in self.sem}
        ns = {"sp": 12, "act": 4, "pool": 8}
        self.dsem = {k: [es.enter_context(nc.semaphore(f"d_{k}{i}")) for i in range(n)] for k, n in ns.items()}
        self.dcnt = {k: [0] * n for k, n in ns.items()}
        self.dnext = {k: 0 for k in ns}
        self.dlast = {k: [None] * n for k, n in ns.items()}
        self.lastw = {}
        self.readers = {}
        self.waited = {k: {} for k in self.ENGS}
        self.latest = {}
        self.pending = {k: [] for k in self.ENGS}
        self.nins = 0

    def op(self, eng, fn, reads=(), writes=(), dma=False):
        deps = []
        for t in reads:
            e = self.lastw.get(t)
            if e is not None:
                deps.append(e)
        for t in writes:
            e = self.lastw.get(t)
            if e is not None:
                deps.append(e)
            deps.extend(self.readers.get(t, ()))
        if self.pending[eng]:
            deps.extend(self.pending[eng])
            self.pending[eng] = []
        if dma:
            slot = self.dnext[eng]
            self.dnext[eng] = (slot + 1) % len(self.dsem[eng])
            if self.dlast[eng][slot] is not None:
                deps.append(self.dlast[eng][slot])
            self.dcnt[eng][slot] += 1
            ev = Ev(("d", eng, slot), self.dsem[eng][slot], 16 * self.dcnt[eng][slot])
            self.dlast[eng][slot] = ev
        else:
            self.cnt[eng] += 1
            ev = Ev((eng,), self.sem[eng], self.cnt[eng])
        waits = {}
        for d in deps:
            if d.key == (eng,) and eng == "pe":
                continue
            if self.waited[eng].get(d.key, 0) >= d.val:
                continue
            if d.key not in waits or waits[d.key].val < d.val:
                waits[d.key] = d
        for k, d in waits.items():
            self.waited[eng][k] = d.val
        self.q[eng].append((list(waits.values()), fn, ev, dma))
        self.latest[ev.key] = ev
        for t in reads:
            self.readers.setdefault(t, []).append(ev)
        for t in writes:
            self.lastw[t] = ev
            self.readers[t] = []
        self.nins += 1
        return ev

    def barrier(self):
        snap = list(self.latest.values())
        for k in self.ENGS:
            self.pending[k] = list(snap)

    def finish(self, eng="sp"):
        waits = []
        for k, d in self.latest.items():
            if self.waited[eng].get(k, 0) < d.val:
                waits.append(d)
        self.q[eng].append((waits, None, None, False))

    def emit(self, block):
        def mk(k):
            def f(e):
                for waits, fn, ev, dma in self.q[k]:
                    for w in waits:
                        e.wait_ge(w.sem, w.val)
                    if fn is None:
                        continue
                    ins = fn(e)
                    ins.then_inc(ev.sem, 16 if dma else 1)
            return f
        block.tensor(mk("pe"))
        block.scalar(mk("act"))
        block.vector(mk("dve"))
        block.gpsimd(mk("pool"))
        block.sync(mk("sp"))


L = 2048
D = 1024
NT = 16
NG = 4
FH = 2816
NHC = FH // 128
EPS = 1e-6
TWO_PI = 6.283185
ARENA_W = 30950
GELU_TANH = AF.Gelu_apprx_tanh


class Ctx:
    pass


def _bf(ap_f32):
    return ap_f32.bitcast(BF16)


class Alloc:
    def __init__(self, arena, base=0, limit=ARENA_W):
        self.a, self.p, self.limit = arena, base, limit

    def f32(self, n):
        ap = self.a[:, self.p:self.p + n]
        self.p += n
        assert self.p <= self.limit, ("arena overflow", self.p)
        return ap

    def bf16(self, n):
        w = (n + 1) // 2
        return _bf(self.f32(w))

    def i32(self, n):
        return self.f32(n).bitcast(I32)


def build_nc(plan=(("mix", 0), ("ffn", 0), ("mix", 1), ("ffn", 1)), dbg=None):
    nc = bass.Bass("TRN2", target_bir_lowering=False)
    K = Ctx()
    K.nc = nc
    dr = {}

    def din(name, shape, dt=F32):
        dr[name] = nc.dram_tensor(name, list(shape), dt, kind="ExternalInput").ap()
        return dr[name]

    din("x", [L, D]); din("ccol", [128, 8]); din("pos", [1, L], I32)
    din("mod_w", [2, D, 6 * D]); din("mod_b", [2, 1, 6 * D]); din("norm_g", [8, 1, D])
    din("w_out", [2, D, D]); din("w_gate", [2, D, FH]); din("w_up", [2, D, FH]); din("w_down", [2, FH, D])
    din("e_w_in", [D, 1184]); din("qg_col", [128, 3]); din("kvg_col", [128, 2])
    din("w_q_up", [384, 768]); din("w_kv_up", [256, 1024])
    din("s5_lam_re", [128, 4, 64]); din("s5_lam_im", [128, 4, 64]); din("s5_logdt", [128, 4, 64])
    din("s5_b_re", [128, 4, 64]); din("s5_b_im", [128, 4, 64])
    din("s5_thp", [128, 32, 3])
    din("s5_c1", [128, 32, 16]); din("s5_c2", [128, 32, 16])
    din("s5_d_col", [128, 4]); din("s5_w_glu", [512, 512]); din("s5_bglu_col", [128, 4])
    din("o_w_in", [D, 2560]); din("conv_w_col", [128, 4, 4]); din("conv_b_col", [128, 4])
    din("lru_wa", [4, 128, 128]); din("lru_wx", [4, 128, 128])
    din("lru_ba_col", [128, 4]); din("lru_bx_col", [128, 4]); din("lru_lam_col", [128, 4])
    din("cst_f32", [128, 1024]); din("cst_bf", [128, 1024], BF16)
    out_d = nc.dram_tensor("out", [L, D], F32, kind="ExternalOutput").ap()
    s5u_d = nc.dram_tensor("s5u_scr", [4, 128, L], BF16, kind="Internal").ap()
    K.dr, K.out_d, K.s5u_d = dr, out_d, s5u_d
    K.dbg = []
    K.dbgsel = dbg or ()

    with ExitStack() as es:
        Hh = es.enter_context(nc.sbuf_tensor("H", [128, NT * D], F32))
        CF = es.enter_context(nc.sbuf_tensor("CF", [128, 768], F32))
        CB = es.enter_context(nc.sbuf_tensor("CB", [128, 1024], BF16))
        MODS = es.enter_context(nc.sbuf_tensor("MODS", [128, 2 * 2048 + 64 + 64], F32))
        A = es.enter_context(nc.sbuf_tensor("ARENA", [128, ARENA_W], F32))
        ps = [es.enter_context(nc.psum_tensor(f"ps{i}", [128, 512], F32)) for i in range(8)]
        S = Sched(nc, es)
        block = es.enter_context(nc.Block())
        K.S, K.A, K.ps = S, A, ps
        K.H3 = Hh[:, :].rearrange("p (t f) -> p t f", t=NT)
        K.identf = CF[:, 0:128]; K.swapf = CF[:, 128:256]; K.iota = CF[:, 256:384]
        K.sel_e = CF[:, 384:512]; K.sel_o = CF[:, 512:640]; K.invrow = CF[0:1, 640:736]
        K.rowmask = CF[:, 736:744]; K.sign1 = CF[:, 744:745]
        K.identb = CB[:, 0:128]; K.onesb = CB[:, 128:256]; K.trib = CB[:, 256:384]
        K.mle = CB[:, 384:512]; K.mlt = CB[:, 512:640]
        K.gmix = [MODS[:, 0:1024], MODS[:, 2048:3072]]
        K.gffn = [MODS[:, 1024:2048], MODS[:, 3072:4096]]
        K.cols = [MODS[:, 4096:4128], MODS[:, 4128:4160]]
        K.small = MODS[:, 4160:4224]

        S.op("sp", lambda e: e.dma_start(out=CF[:, :], in_=dr["cst_f32"][:, 0:768]), writes=["CF"], dma=True)
        S.op("sp", lambda e: e.dma_start(out=CB[:, :], in_=dr["cst_bf"]), writes=["CB"], dma=True)
        for t in range(NT):
            S.op("sp", lambda e, t=t: e.dma_start(out=K.H3[:, t, :], in_=dr["x"][128 * t:128 * t + 128, :]),
                 writes=[("H", t)], dma=True)
        stage_mods(K)
        for kind, l in plan:
            S.barrier()
            if kind == "ffn":
                stage_prenorm(K, l, 1)
                stage_ffn(K, l)
            elif kind == "mix":
                stage_prenorm(K, l, 0)
                if l % 2 == 0:
                    stage_even(K, l)
                else:
                    stage_odd(K, l)
                stage_outproj(K, l)
        S.barrier()
        for t in range(NT):
            S.op("sp", lambda e, t=t: e.dma_start(out=out_d[128 * t:128 * t + 128, :], in_=K.H3[:, t, :]),
                 reads=[("H", t)], dma=True)
        if "mods" in K.dbgsel:
            K.dbg += [("cols0", K.cols[0]), ("cols1", K.cols[1]), ("gffn0", K.gffn[0]), ("gmix0", K.gmix[0])]
        if "uT" in K.dbgsel:
            K.dbg += [("uT%d" % c, K.uT[:, c, :]) for c in (0, 7)]
        for name, ap in K.dbg:
            dd = nc.dram_tensor("dbg_" + name, list(ap.shape), F32, kind="ExternalOutput").ap()
            S.op("pool", lambda e, dd=dd, ap=ap: e.dma_start(out=dd, in_=ap), dma=True)
        S.finish("sp")
        S.emit(block)
    return nc


def stage_mods(K):
    S, dr, ps = K.S, K.dr, K.ps
    al = Alloc(K.A)
    ccol = al.f32(8); cact = al.f32(8)
    cbc = al.f32(1024).rearrange("p (k m) -> p k m", k=8)
    wst = [al.f32(4096).rearrange("p (k n) -> p k n", k=8) for _ in range(2)]
    modb = al.f32(6144)
    ng = [al.f32(1024) for _ in range(4)]
    tmp = [al.f32(512) for _ in range(2)]
    junk = al.f32(128)
    S.op("sp", lambda e: e.dma_start(out=ccol, in_=dr["ccol"]), writes=["ccol"], dma=True)
    S.op("act", lambda e: e.activation(out=cact, in_=ccol, func=AF.Silu), reads=["ccol"], writes=["cact"])
    for k in range(8):
        S.op("dve", lambda e, k=k: e.tensor_copy(out=cbc[:, k, :], in_=cact[:, k:k + 1].to_broadcast([128, 128])),
             reads=["cact"], writes=["cbc"])
    it = 0
    for l in range(2):
        S.op("sp", lambda e, l=l: e.dma_start(out=modb, in_=dr["mod_b"][l].partition_broadcast(128)),
             writes=["modb"], dma=True)
        for n in range(4):
            S.op("sp", lambda e, l=l, n=n: e.dma_start(out=ng[n], in_=dr["norm_g"][4 * l + n].partition_broadcast(128)),
                 writes=[("ng", n)], dma=True)
        mw = dr["mod_w"][l].rearrange("(k p) n -> p k n", p=128)
        for grp in range(12):
            b = it % 2
            pb = ps[it % 2]
            it += 1
            for hk in range(2):
                S.op("sp" if hk == 0 else "act", lambda e, b=b, grp=grp, hk=hk, mw=mw: e.dma_start(
                    out=wst[b][:, 4 * hk:4 * hk + 4, :], in_=mw[:, 4 * hk:4 * hk + 4, 512 * grp:512 * grp + 512]),
                    writes=[("wst", b, hk)], dma=True)
            for k in range(8):
                S.op("pe", lambda e, b=b, k=k, pb=pb: e.matmul(pb[:, :], lhsT=cbc[:, k, :], rhs=wst[b][:, k, :],
                                                               start=(k == 0), stop=(k == 7)),
                     reads=["cbc", ("wst", b, k // 4)], writes=[("ps", b)])
            kind, half = grp // 2, grp % 2
            cs = slice(512 * half, 512 * half + 512)
            tb = tmp[b]
            S.op("dve", lambda e, pb=pb, tb=tb, grp=grp: e.tensor_tensor(out=tb, in0=pb[:, :], in1=modb[:, 512 * grp:512 * grp + 512], op=ALU.add),
                 reads=[("ps", b), "modb"], writes=[("mtmp", b)])
            if kind in (1, 4):
                g_ = ng[0 if kind == 1 else 2]
                S.op("dve", lambda e, tb=tb, g_=g_, cs=cs: e.scalar_tensor_tensor(out=tb, in0=tb, scalar=1.0, in1=g_[:, cs], op0=ALU.add, op1=ALU.mult),
                     reads=[("mtmp", b), ("ng", 0 if kind == 1 else 2)], writes=[("mtmp", b)])
            if kind in (0, 1, 3, 4):
                off = {1: 0, 0: 8, 4: 16, 3: 24}[kind]
                for q in range(4):
                    col = K.cols[l][:, off + 4 * half + q: off + 4 * half + q + 1]
                    S.op("dve", lambda e, tb=tb, q=q: e.tensor_tensor(out=junk, in0=tb[:, 128 * q:128 * q + 128], in1=K.identf, op=ALU.mult),
                         reads=[("mtmp", b), "CF"], writes=["mjunk"])
                    S.op("dve", lambda e, col=col: e.tensor_reduce(out=col, in_=junk, axis=AX.X, op=ALU.add),
                         reads=["mjunk"], writes=[("cols", l)])
            else:
                dst = (K.gmix if kind == 2 else K.gffn)[l]
                g_ = ng[1 if kind == 2 else 3]
                S.op("dve", lambda e, tb=tb, dst=dst, g_=g_, cs=cs: e.tensor_tensor(out=dst[:, cs], in0=tb, in1=g_[:, cs], op=ALU.mult),
                     reads=[("mtmp", b), ("ng", 1 if kind == 2 else 3)], writes=["gate"])


def stage_prenorm(K, l, which):
    S, ps = K.S, K.ps
    al = Alloc(K.A)
    K.uT = al.bf16(8 * L).rearrange("p (c t) -> p c t", c=8)
    K.after_uT = al.p
    xn = [al.bf16(1024) for _ in range(8)]
    junk = al.bf16(1024)
    ss = K.small[:, 0:16]; sq = K.small[:, 16:32]; rstd = K.small[:, 32:48]
    sc = K.cols[l][:, 16 * which: 16 * which + 8]
    sh = K.cols[l][:, 16 * which + 8: 16 * which + 16]
    for t in range(NT):
        S.op("act", lambda e, t=t: e.activation(out=junk, in_=K.H3[:, t, :], func=AF.Square, accum_out=ss[:, t:t + 1]),
             reads=[("H", t)], writes=["pn_junk", "pn_ss"])
    S.op("act", lambda e: e.activation(out=sq, in_=ss, func=AF.Sqrt, scale=1.0 / D, bias=EPS), reads=["pn_ss"], writes=["pn_sq"])
    S.op("dve", lambda e: e.reciprocal(out=rstd, in_=sq), reads=["pn_sq"], writes=["pn_rstd"])
    ev = 0
    for tg in range(NG):
        for i in range(4):
            t = 4 * tg + i
            b = t % 8
            S.op("act", lambda e, t=t, b=b: e.activation(out=xn[b], in_=K.H3[:, t, :], func=AF.Copy, scale=rstd[:, t:t + 1]),
                 reads=[("H", t), "pn_rstd"], writes=[("xn", b)])
        for c in range(8):
            pi = 2 + (ev % 2)
            pb = _bf(ps[pi][:, 0:256])
            for i in range(4):
                b = (4 * tg + i) % 8
                S.op("pe", lambda e, pb=pb, i=i, b=b, c=c: e.transpose(out=pb[:, 128 * i:128 * i + 128], in_=xn[b][:, 128 * c:128 * c + 128], identity=K.identb),
                     reads=[("xn", b), "CB"], writes=[("ps", pi)])
            dst = K.uT[:, c, 512 * tg:512 * tg + 512]
            if ev % 2 == 0:
                S.op("dve", lambda e, pb=pb, dst=dst, c=c: e.tensor_scalar(out=dst, in0=pb, scalar1=sc[:, c:c + 1], scalar2=sh[:, c:c + 1], op0=ALU.mult, op1=ALU.add),
                     reads=[("ps", pi), ("cols", l)], writes=[("uT", c, tg)])
            else:
                S.op("act", lambda e, pb=pb, dst=dst, c=c: e.activation(out=dst, in_=pb, func=AF.Identity, scale=sc[:, c:c + 1], bias=sh[:, c:c + 1]),
                     reads=[("ps", pi), ("cols", l)], writes=[("uT", c, tg)])
            ev += 1


def epilogue_tile(K, l, t, pA, pB, gate, tmp2):
    S, ps = K.S, K.ps
    ssA = K.small[:, 48:49]; ssB = K.small[:, 49:50]; s1 = K.small[:, 50:51]; s2 = K.small[:, 51:52]; r1 = K.small[:, 52:53]
    junk = tmp2[0]
    S.op("act", lambda e: e.activation(out=junk, in_=ps[pA][:, :], func=AF.Square, accum_out=ssA), reads=[("ps", pA)], writes=["ep_junk", "ep_ssA"])
    S.op("act", lambda e: e.activation(out=junk, in_=ps[pB][:, :], func=AF.Square, accum_out=ssB), reads=[("ps", pB)], writes=["ep_junk", "ep_ssB"])
    S.op("dve", lambda e: e.tensor_tensor(out=s1, in0=ssA, in1=ssB, op=ALU.add), reads=["ep_ssA", "ep_ssB"], writes=["ep_s1"])
    S.op("act", lambda e: e.activation(out=s2, in_=s1, func=AF.Sqrt, scale=1.0 / D, bias=EPS), reads=["ep_s1"], writes=["ep_s2"])
    S.op("dve", lambda e: e.reciprocal(out=r1, in_=s2), reads=["ep_s2"], writes=["ep_r1"])
    for n, pi in enumerate((pA, pB)):
        cs = slice(512 * n, 512 * n + 512)
        tm = tmp2[1 + n]
        S.op("dve", lambda e, pi=pi, tm=tm, cs=cs: e.scalar_tensor_tensor(out=tm, in0=ps[pi][:, :], scalar=r1, in1=gate[:, cs], op0=ALU.mult, op1=ALU.mult),
             reads=[("ps", pi), "ep_r1", "gate"], writes=[("ep_tm", n)])
        S.op("pool", lambda e, tm=tm, cs=cs, t=t: e.tensor_tensor(out=K.H3[:, t, cs], in0=K.H3[:, t, cs], in1=tm, op=ALU.add),
             reads=[("ep_tm", n), ("H", t)], writes=[("H", t)])


def stage_ffn(K, l, nhalf=4):
    S, ps, dr = K.S, K.ps, K.dr
    S.barrier()
    al = Alloc(K.A, K.after_uT)
    TH = L // nhalf
    hT = al.bf16(NHC * TH).rearrange("p (c t) -> p c t", c=NHC)
    wd = al.bf16(NHC * D).rearrange("p (c n) -> p c n", c=NHC)
    wg = [al.bf16(1024).rearrange("p (k m) -> p k m", k=8) for _ in range(2)]
    wu = [al.bf16(1024).rearrange("p (k m) -> p k m", k=8) for _ in range(2)]
    sil = [al.f32(512) for _ in range(2)]
    tmp2 = [al.f32(512) for _ in range(3)]
    wgd = dr["w_gate"][l].rearrange("(k p) n -> p k n", p=128)
    wud = dr["w_up"][l].rearrange("(k p) n -> p k n", p=128)
    wdd = dr["w_down"][l].rearrange("(c p) n -> p c n", p=128)
    for c in range(NHC):
        S.op("pool", lambda e, c=c: e.dma_start(out=wd[:, c, :], in_=wdd[:, c, :]), writes=[("wd", c)], dma=True)
    it = 0
    for hf in range(nhalf):
        for c in range(NHC):
            b = it % 2
            it += 1
            S.op("pool", lambda e, b=b, c=c: e.dma_start(out=wg[b], in_=wgd[:, :, 128 * c:128 * c + 128]), writes=[("wg", b)], dma=True)
            S.op("pool", lambda e, b=b, c=c: e.dma_start(out=wu[b], in_=wud[:, :, 128 * c:128 * c + 128]), writes=[("wu", b)], dma=True)
            for tg in range(TH // 512):
                tgg = hf * (TH // 512) + tg
                pg, pu = (0, 1) if (tg % 2 == 0) else (2, 3)
                for k in range(8):
                    S.op("pe", lambda e, b=b, k=k, pg=pg, tgg=tgg: e.matmul(ps[pg][:, :], lhsT=wg[b][:, k, :], rhs=K.uT[:, k, 512 * tgg:512 * tgg + 512], start=(k == 0), stop=(k == 7)),
                         reads=[("wg", b), ("uT", k, tgg)], writes=[("ps", pg)])
                for k in range(8):
                    S.op("pe", lambda e, b=b, k=k, pu=pu, tgg=tgg: e.matmul(ps[pu][:, :], lhsT=wu[b][:, k, :], rhs=K.uT[:, k, 512 * tgg:512 * tgg + 512], start=(k == 0), stop=(k == 7)),
                         reads=[("wu", b), ("uT", k, tgg)], writes=[("ps", pu)])
                sb = sil[tg % 2]
                S.op("act", lambda e, sb=sb, pg=pg: e.activation(out=sb, in_=ps[pg][:, :], func=AF.Silu), reads=[("ps", pg)], writes=[("sil", tg % 2)])
                S.op("dve", lambda e, sb=sb, pu=pu, c=c, tg=tg: e.tensor_tensor(out=hT[:, c, 512 * tg:512 * tg + 512], in0=sb, in1=ps[pu][:, :], op=ALU.mult),
                     reads=[("sil", tg % 2), ("ps", pu)], writes=[("hT", c)])
        for tt in range(TH // 128):
            t = hf * (TH // 128) + tt
            pA, pB = (4, 5) if (tt % 2 == 0) else (6, 7)
            for n, pi in enumerate((pA, pB)):
                for c in range(NHC):
                    S.op("pe", lambda e, c=c, tt=tt, n=n, pi=pi: e.matmul(ps[pi][:, :], lhsT=hT[:, c, 128 * tt:128 * tt + 128], rhs=wd[:, c, 512 * n:512 * n + 512], start=(c == 0), stop=(c == NHC - 1)),
                         reads=[("hT", c), ("wd", c)], writes=[("ps", pi)])
            epilogue_tile(K, l, t, pA, pB, K.gffn[l], tmp2)


def stage_outproj(K, l):
    S, ps, dr = K.S, K.ps, K.dr
    S.barrier()
    al = Alloc(K.A, K.op_base)
    wo = al.bf16(8 * D).rearrange("p (c n) -> p c n", c=8)
    tmp2 = [al.f32(512) for _ in range(3)]
    wod = dr["w_out"][l].rearrange("(c p) n -> p c n", p=128)
    for c in range(8):
        S.op("pool", lambda e, c=c: e.dma_start(out=wo[:, c, :], in_=wod[:, c, :]), writes=[("wo", c)], dma=True)
    for t in range(NT):
        pA, pB = (4, 5) if (t % 2 == 0) else (6, 7)
        for n, pi in enumerate((pA, pB)):
            for c in range(8):
                S.op("pe", lambda e, c=c, t=t, n=n, pi=pi: e.matmul(ps[pi][:, :], lhsT=K.mT[:, c, 128 * t:128 * t + 128], rhs=wo[:, c, 512 * n:512 * n + 512], start=(c == 0), stop=(c == 7)),
                     reads=[("mT", c), ("wo", c)], writes=[("ps", pi)])
        epilogue_tile(K, l, t, pA, pB, K.gmix[l], tmp2)


def _consts():
    cf = np.zeros((128, 1024), np.float32)
    cf[:, 0:128] = np.eye(128)
    sw = np.zeros((128, 128), np.float32)
    for m in range(64):
        sw[m + 64, m] = -1.0
        sw[m, m + 64] = 1.0
    cf[:, 128:256] = sw
    cf[:, 256:384] = np.arange(128, dtype=np.float32)[None, :]
    cf[64, 384:448] = 1.0
    cf[0, 512 + 64:512 + 128] = 1.0
    inv = (10000.0 ** (-np.arange(16, dtype=np.float64) / 16.0)) / (2 * np.pi)
    cf[0, 640 + 64:640 + 96] = np.concatenate([inv, inv]).astype(np.float32)
    for gl in range(8):
        cf[16 * gl:16 * gl + 16, 736 + gl] = 1.0
    cf[0:64, 744] = 1.0
    cf[64:128, 744] = -1.0
    cb = np.zeros((128, 1024), np.float32)
    cb[:, 0:128] = np.eye(128)
    cb[:, 128:256] = 1.0
    j = np.arange(128)[:, None]; s_ = np.arange(128)[None, :]
    cb[:, 256:384] = (j > s_)
    cb[:, 384:512] = (j <= s_)
    cb[:, 512:640] = (j < s_)
    return cf, cb.astype(ml_dtypes.bfloat16)


def _col(v, k):
    return np.ascontiguousarray(np.asarray(v, np.float32).reshape(k, 128).T)


def prep_core(inp, b):
    f = lambda a: np.ascontiguousarray(np.asarray(a, np.float32))
    cf, cb = _consts()
    m = {}
    m["x"] = f(inp["x"][b]); m["ccol"] = _col(inp["c"][b], 8)
    m["pos"] = np.ascontiguousarray(np.asarray(inp["positions"][b], np.int32).reshape(1, L))
    m["mod_w"] = f(inp["mod_w"]); m["mod_b"] = f(inp["mod_b"]).reshape(2, 1, 6 * D)
    m["norm_g"] = f(inp["norm_g"]).reshape(8, 1, D)
    m["w_out"] = f(inp["w_out"]); m["w_gate"] = f(inp["ffn_w_gate"]); m["w_up"] = f(inp["ffn_w_up"]); m["w_down"] = f(inp["ffn_w_down"])
    m["e_w_in"] = f(inp["even_w_in"][0]); m["qg_col"] = _col(inp["mla_q_norm_g"][0], 3); m["kvg_col"] = _col(inp["mla_kv_norm_g"][0], 2)
    m["w_q_up"] = f(inp["mla_w_q_up"][0]); m["w_kv_up"] = f(inp["mla_w_kv_up"][0])
    def gl_layout(a):
        a = np.asarray(a, np.float32).reshape(4, 8, 64).transpose(1, 0, 2)
        return np.ascontiguousarray(np.repeat(a[:, None], 16, axis=1).reshape(128, 4, 64))
    m["s5_lam_re"] = gl_layout(inp["s5_lam_re"][0]); m["s5_lam_im"] = gl_layout(inp["s5_lam_im"][0])
    m["s5_logdt"] = gl_layout(np.repeat(np.asarray(inp["s5_log_dt"][0], np.float32)[:, None], 64, axis=1))
    def b_layout(a):
        a = np.asarray(a, np.float32).reshape(4, 8, 64, 16).transpose(1, 3, 0, 2)
        return np.ascontiguousarray(a.reshape(128, 4, 64))
    m["s5_b_re"] = b_layout(inp["s5_b_re"][0]); m["s5_b_im"] = b_layout(inp["s5_b_im"][0])
    lr = np.asarray(inp["s5_lam_re"][0], np.float32).T; li = np.asarray(inp["s5_lam_im"][0], np.float32).T
    ld = np.repeat(np.asarray(inp["s5_log_dt"][0], np.float32)[None, :], 64, axis=0)
    thp = np.stack([np.concatenate([lr, lr], 0), np.concatenate([li, li], 0), np.concatenate([ld, ld], 0)], axis=-1)
    m["s5_thp"] = np.ascontiguousarray(thp.astype(np.float32))
    cr = np.asarray(inp["s5_c_re"][0], np.float32).transpose(2, 0, 1); ci = np.asarray(inp["s5_c_im"][0], np.float32).transpose(2, 0, 1)
    m["s5_c1"] = np.ascontiguousarray(np.concatenate([cr, ci], 0)); m["s5_c2"] = np.ascontiguousarray(np.concatenate([ci, cr], 0))
    m["s5_d_col"] = _col(np.asarray(inp["s5_d"][0]).reshape(512), 4); m["s5_w_glu"] = f(inp["s5_w_glu"][0]); m["s5_bglu_col"] = _col(inp["s5_b_glu"][0], 4)
    m["o_w_in"] = f(inp["odd_w_in"][0])
    cw = np.asarray(inp["lru_conv_w"][0], np.float32)
    m["conv_w_col"] = np.ascontiguousarray(cw.reshape(4, 4, 128).transpose(2, 1, 0))
    m["conv_b_col"] = _col(inp["lru_conv_b"][0], 4)
    def bd(w):
        w = np.asarray(w, np.float32); o = np.zeros((4, 128, 128), np.float32)
        for n in range(8):
            o[n // 2, 64 * (n % 2):64 * (n % 2) + 64, 64 * (n % 2):64 * (n % 2) + 64] = w[n]
        return o
    m["lru_wa"] = bd(inp["lru_w_a"][0]); m["lru_wx"] = bd(inp["lru_w_x"][0])
    m["lru_ba_col"] = _col(inp["lru_b_a"][0], 4); m["lru_bx_col"] = _col(inp["lru_b_x"][0], 4); m["lru_lam_col"] = _col(inp["lru_lambda"][0], 4)
    m["cst_f32"] = cf; m["cst_bf"] = cb
    return m


_NC_CACHE = {}
LAUNCH_PLANS = ((("mix", 0),), (("ffn", 0),), (("mix", 1), ("ffn", 1)))


def kernel(**inputs):
    inputs = {k: np.asarray(v) for k, v in inputs.items()}
    in_maps = [prep_core(inputs, b) for b in range(8)]
    for plan in LAUNCH_PLANS:
        if plan not in _NC_CACHE:
            _NC_CACHE[plan] = build_nc(plan=plan)
        res = run_bass_kernel_spmd(_NC_CACHE[plan], in_maps, core_ids=list(range(8)))
        for b in range(8):
            in_maps[b]["x"] = np.ascontiguousarray(np.asarray(res.results[b]["out"], np.float32))
    out = np.stack([in_maps[b]["x"] for b in range(8)], axis=0)
    return out.astype(np.float32)


SB_SCALE = 0.125


def stage_odd(K, l):
    S, ps, dr, nc = K.S, K.ps, K.dr, K.nc
    S.barrier()
    A = K.A
    K.mT = _bf(A[:, 8192:16384]).rearrange("p (c t) -> p c t", c=8)
    K.op_base = 16384
    al = Alloc(A, 16384)
    QT = al.bf16(4 * L).rearrange("p (c t) -> p c t", c=4)
    KT = al.bf16(4 * L).rearrange("p (c t) -> p c t", c=4)
    V = al.bf16(NT * 512).rearrange("p (t f) -> p t f", t=NT)
    wfull = al.bf16(8 * 512)
    wst = [wfull[:, 1024 * i:1024 * i + 1024].rearrange("p (k m) -> p k m", k=8) for i in range(4)]
    wv = wfull.rearrange("p (k m) -> p k m", k=8)
    tail0 = al.p
    if not hasattr(K, "xr_d"):
        K.xr_d = nc.dram_tensor("xr_scr", [4, 128, L], F32, kind="Internal").ap()
        K.yg_d = nc.dram_tensor("yg_scr", [4, 128, L], F32, kind="Internal").ap()
    wd_ = dr["o_w_in"].rearrange("(k p) n -> p k n", p=128)
    stg = [A[:, 8192:8704]]
    it = 0
    chunks = [("q", c) for c in range(4)] + [("k", c) for c in range(4)] + [("xr", c) for c in range(4)] + [("yg", c) for c in range(4)]
    colbase = {"q": 0, "k": 512, "xr": 1536, "yg": 2048}
    evn = 0
    for kind, c in chunks:
        b = it % 4
        it += 1
        c0 = colbase[kind] + 128 * c
        S.op("pool", lambda e, b=b, c0=c0: e.dma_start(out=wst[b], in_=wd_[:, :, c0:c0 + 128]), writes=[("wst", b)], dma=True)
        for tg in range(NG):
            pi = evn % 4
            evn += 1
            for k in range(8):
                S.op("pe", lambda e, b=b, k=k, pi=pi, tg=tg: e.matmul(ps[pi][:, :], lhsT=wst[b][:, k, :], rhs=K.uT[:, k, 512 * tg:512 * tg + 512], start=(k == 0), stop=(k == 7)),
                     reads=[("wst", b), ("uT", k, tg)], writes=[("ps", pi)])
            if kind in ("q", "k"):
                dst = (QT if kind == "q" else KT)[:, c, 512 * tg:512 * tg + 512]
                eng = "act" if evn % 2 == 0 else "dve"
                if eng == "act":
                    S.op("act", lambda e, dst=dst, pi=pi: e.activation(out=dst, in_=ps[pi][:, :], func=AF.Copy), reads=[("ps", pi)], writes=[(kind, c)])
                else:
                    S.op("dve", lambda e, dst=dst, pi=pi: e.tensor_copy(out=dst, in_=ps[pi][:, :]), reads=[("ps", pi)], writes=[(kind, c)])
            else:
                S.op("act", lambda e, pi=pi: e.activation(out=stg[0], in_=ps[pi][:, :], func=AF.Copy), reads=[("ps", pi)], writes=["ostg"])
                dd = (K.xr_d if kind == "xr" else K.yg_d)[c][:, 512 * tg:512 * tg + 512]
                S.op("sp", lambda e, dd=dd: e.dma_start(out=dd, in_=stg[0]), reads=["ostg"], writes=[(kind + "_d", c)], dma=True)
    S.op("pool", lambda e: e.dma_start(out=wv, in_=wd_[:, :, 1024:1536]), writes=[("wst", 0), ("wst", 1), ("wst", 2), ("wst", 3)], dma=True)
    for t in range(NT):
        pi = 4 + t % 2
        for k in range(8):
            S.op("pe", lambda e, k=k, pi=pi, t=t: e.matmul(ps[pi][:, :], lhsT=K.uT[:, k, 128 * t:128 * t + 128], rhs=wv[:, k, :], start=(k == 0), stop=(k == 7)),
                 reads=[("wst", 0), ("uT", k, t // 4)], writes=[("ps", pi)])
        if t % 2 == 0:
            S.op("act", lambda e, pi=pi, t=t: e.activation(out=V[:, t, :], in_=ps[pi][:, :], func=AF.Copy), reads=[("ps", pi)], writes=["V"])
        else:
            S.op("dve", lambda e, pi=pi, t=t: e.tensor_copy(out=V[:, t, :], in_=ps[pi][:, :]), reads=[("ps", pi)], writes=["V"])
    S.barrier()
    al2 = Alloc(A, 0, 8192)
    B0 = al2.f32(2052); B1 = al2.f32(2048); B3 = al2.f32(2048)
    B4 = A[:, 8192:10240]
    al3 = Alloc(A, tail0 - 2048)
    B2 = _bf(A[:, 10240:11264])
    WA = al3.bf16(128); WX = al3.bf16(128)
    prm = al3.f32(64)
    cw = prm[:, 0:16].rearrange("p (c w) -> p c w", c=4); cb = prm[:, 16:20]; ba = prm[:, 20:24]; bx = prm[:, 24:28]
    lam = prm[:, 28:32]; colA = prm[:, 32:36]; ltmp = prm[:, 36:40]
    S.op("sp", lambda e: e.dma_start(out=prm[:, 0:16], in_=dr["conv_w_col"].rearrange("p c w -> p (c w)")), writes=["prm"], dma=True)
    for nm, ap in (("conv_b_col", cb), ("lru_ba_col", ba), ("lru_bx_col", bx), ("lru_lam_col", lam)):
        S.op("sp", lambda e, nm=nm, ap=ap: e.dma_start(out=ap, in_=dr[nm]), writes=["prm"], dma=True)
    S.op("act", lambda e: e.activation(out=ltmp, in_=lam, func=AF.Exp, scale=-1.0), reads=["prm"], writes=["ltmp"])
    S.op("act", lambda e: e.activation(out=ltmp, in_=ltmp, func=AF.Ln, bias=1.0), reads=["ltmp"], writes=["ltmp"])
    S.op("dve", lambda e: e.tensor_scalar(out=colA, in0=ltmp, scalar1=-8.0, scalar2=None, op0=ALU.mult), reads=["ltmp"], writes=["colA"])
    S.op("dve", lambda e: e.memset(B0[:, 0:3], 0.0), writes=["B0"])
    for c in range(4):
        S.op("pool", lambda e, c=c: e.dma_start(out=WA, in_=dr["lru_wa"][c]), writes=["WA"], dma=True)
        S.op("pool", lambda e, c=c: e.dma_start(out=WX, in_=dr["lru_wx"][c]), writes=["WX"], dma=True)
        S.op("sp", lambda e, c=c: e.dma_start(out=B0[:, 3:2051], in_=K.xr_d[c]), reads=[("xr_d", c)], writes=["B0"], dma=True)
        S.op("dve", lambda e, c=c: e.tensor_scalar(out=B1, in0=B0[:, 3:2051], scalar1=cw[:, c, 3:4], scalar2=cb[:, c:c + 1], op0=ALU.mult, op1=ALU.add),
             reads=["B0", "prm"], writes=["B1"])
        for w in range(3):
            S.op("dve", lambda e, c=c, w=w: e.scalar_tensor_tensor(out=B1, in0=B0[:, w:w + 2048], scalar=cw[:, c, w:w + 1], in1=B1, op0=ALU.mult, op1=ALU.add),
                 reads=["B0", "B1", "prm"], writes=["B1"])
        S.op("act", lambda e: e.activation(out=B2, in_=B1, func=AF.Copy), reads=["B1"], writes=["B2"])
        for tg in range(NG):
            cs = slice(512 * tg, 512 * tg + 512)
            pr, px = (0, 1) if tg % 2 == 0 else (2, 3)
            S.op("pe", lambda e, cs=cs, pr=pr: e.matmul(ps[pr][:, :], lhsT=WA, rhs=B2[:, cs], start=True, stop=True), reads=["WA", "B2"], writes=[("ps", pr)])
            S.op("pe", lambda e, cs=cs, px=px: e.matmul(ps[px][:, :], lhsT=WX, rhs=B2[:, cs], start=True, stop=True), reads=["WX", "B2"], writes=[("ps", px)])
            S.op("act", lambda e, cs=cs, pr=pr, c=c: e.activation(out=B3[:, cs], in_=ps[pr][:, :], func=AF.Sigmoid, bias=ba[:, c:c + 1]), reads=[("ps", pr), "prm"], writes=["B3"])
            S.op("act", lambda e, cs=cs, px=px, c=c: e.activation(out=B4[:, cs], in_=ps[px][:, :], func=AF.Sigmoid, bias=bx[:, c:c + 1]), reads=[("ps", px), "prm"], writes=["B4"])
        S.op("act", lambda e, c=c: e.activation(out=B3, in_=B3, func=AF.Exp, scale=colA[:, c:c + 1]), reads=["B3", "colA"], writes=["B3"])
        S.op("pool", lambda e: e.tensor_tensor(out=B4, in0=B4, in1=B1, op=ALU.mult), reads=["B4", "B1"], writes=["B4"])
        S.op("dve", lambda e: e.tensor_tensor(out=B1, in0=B3, in1=B3, op=ALU.mult), reads=["B3", "B4"], writes=["B1"])
        S.op("act", lambda e: e.activation(out=B1, in_=B1, func=AF.Sqrt, scale=-1.0, bias=1.0), reads=["B1"], writes=["B1"])
        S.op("dve", lambda e: e.tensor_tensor(out=B1, in0=B1, in1=B4, op=ALU.mult), reads=["B1", "B4"], writes=["B1"])
        S.op("dve", lambda e: e.tensor_tensor_scan(out=B4, data0=B3, data1=B1, initial=0.0, op0=ALU.mult, op1=ALU.add), reads=["B3", "B1"], writes=["B4"])
        S.op("sp", lambda e, c=c: e.dma_start(out=B0[:, 3:2051], in_=K.yg_d[c]), reads=[("yg_d", c), "B1"], writes=["B0"], dma=True)
        S.op("act", lambda e: e.activation(out=B0[:, 3:2051], in_=B0[:, 3:2051], func=GELU_TANH), reads=["B0"], writes=["B0"])
        S.op("dve", lambda e, c=c: e.tensor_tensor(out=K.mT[:, 4 + c, :], in0=B4, in1=B0[:, 3:2051], op=ALU.mult), reads=["B4", "B0"], writes=[("mT", 4 + c)])
    S.barrier()
    al4 = Alloc(A, 0, 8192)
    eb = [al4.f32(512) for _ in range(2)]; spb = [al4.f32(512) for _ in range(2)]
    tm = [al4.f32(512) for _ in range(2)]; tm2 = [al4.f32(512) for _ in range(2)]
    Ln_ = [al4.bf16(512) for _ in range(2)]; Wb = [al4.bf16(512) for _ in range(2)]
    Rb = [al4.f32(512) for _ in range(2)]
    pr_i = 0
    for h in range(8):
        c, r0 = h // 2, 64 * (h % 2)
        rows = slice(r0, r0 + 64)
        for j in range(NG):
            po = 6 + (h * NG + j) % 2
            rb = Rb[(h * NG + j) % 2]
            S.op("pool", lambda e, rb=rb: e.memset(rb, 0.0), writes=[("Rb", (h * NG + j) % 2)])
            nkb = 4 * j + 4
            for kb in range(nkb - 1, -1, -1):
                i = kb - 4 * j
                q0 = 128 * max(i, 0)
                qs = slice(q0, 512)
                b = pr_i % 2
                pr_i += 1
                pz = b; psf = 2 + b; pcs = 4 + b
                rbk = ("Rb", (h * NG + j) % 2)
                S.op("pe", lambda e, pz=pz, rows=rows, c=c, kb=kb, j=j, q0=q0, qs=qs: e.matmul(
                    ps[pz][:, qs], lhsT=KT[rows, c, 128 * kb:128 * kb + 128], rhs=QT[rows, c, 512 * j + q0:512 * j + 512], start=True, stop=True),
                    reads=[("q", c), ("k", c)], writes=[("ps", pz)])
                S.op("act", lambda e, b=b, pz=pz, qs=qs: e.activation(out=eb[b][:, qs], in_=ps[pz][:, qs], func=AF.Exp, scale=-SB_SCALE), reads=[("ps", pz)], writes=[("eb", b)])
                S.op("act", lambda e, b=b, qs=qs: e.activation(out=spb[b][:, qs], in_=eb[b][:, qs], func=AF.Ln, bias=1.0), reads=[("eb", b)], writes=[("spb", b)])
                S.op("dve", lambda e, b=b, pz=pz, qs=qs: e.scalar_tensor_tensor(out=Ln_[b][:, qs], in0=ps[pz][:, qs], scalar=SB_SCALE, in1=spb[b][:, qs], op0=ALU.mult, op1=ALU.add),
                     reads=[("ps", pz), ("spb", b)], writes=[("Ln", b)])
                if i >= 0:
                    S.op("dve", lambda e, b=b, q0=q0: e.tensor_tensor(out=Ln_[b][:, q0:q0 + 128], in0=Ln_[b][:, q0:q0 + 128], in1=K.mlt, op=ALU.mult),
                         reads=[("Ln", b), "CB"], writes=[("Ln", b)])
                S.op("pe", lambda e, b=b, psf=psf, qs=qs: e.matmul(ps[psf][:, qs], lhsT=K.trib, rhs=Ln_[b][:, qs], start=True, stop=True), reads=[("Ln", b), "CB"], writes=[("ps", psf)])
                S.op("pe", lambda e, b=b, pcs=pcs, qs=qs: e.matmul(ps[pcs][:, qs], lhsT=K.onesb, rhs=Ln_[b][:, qs], start=True, stop=True), reads=[("Ln", b), "CB"], writes=[("ps", pcs)])
                S.op("dve", lambda e, b=b, psf=psf, qs=qs: e.tensor_tensor(out=tm[b][:, qs], in0=ps[psf][:, qs], in1=spb[b][:, qs], op=ALU.add),
                     reads=[("ps", psf), ("spb", b)], writes=[("tm", b)])
                S.op("pool", lambda e, b=b, rb=rb, qs=qs: e.tensor_tensor(out=tm2[b][:, qs], in0=tm[b][:, qs], in1=rb[:, qs], op=ALU.add),
                     reads=[("tm", b), rbk], writes=[("tm2", b)])
                S.op("act", lambda e, b=b, qs=qs: e.activation(out=Wb[b][:, qs], in_=tm2[b][:, qs], func=AF.Exp, scale=-1.0), reads=[("tm2", b)], writes=[("W", b)])
                if i >= 0:
                    S.op("dve", lambda e, b=b, q0=q0: e.tensor_tensor(out=Wb[b][:, q0:q0 + 128], in0=Wb[b][:, q0:q0 + 128], in1=K.mlt, op=ALU.mult),
                         reads=[("W", b), "CB"], writes=[("W", b)])
                S.op("dve", lambda e, rb=rb, pcs=pcs, qs=qs: e.tensor_tensor(out=rb[:, qs], in0=rb[:, qs], in1=ps[pcs][:, qs], op=ALU.add),
                     reads=[("ps", pcs), rbk], writes=[rbk])
                S.op("pe", lambda e, b=b, po=po, rows=rows, r0=r0, kb=kb, h=h, qs=qs, nkb=nkb: e.matmul(
                    ps[po][rows, qs], lhsT=V[:, kb, 64 * h:64 * h + 64], rhs=Wb[b][:, qs], start=(kb == nkb - 1), stop=(kb == 0), tile_position=(0, r0), skip_group_check=True),
                    reads=["V", ("W", b)], writes=[("ps", po)])
            dst = K.mT[rows, c, 512 * j:512 * j + 512]
            S.op("act", lambda e, dst=dst, po=po, rows=rows: e.activation(out=dst, in_=ps[po][rows, :], func=AF.Copy), reads=[("ps", po)], writes=[("mT", c)])


MLA_SCALE = 96 ** -0.5
EVEN_MODE = "full"


def range_reduce_sin(K, S, y_ps_ap, dst, shift, wk_i, wk_f, rows, rd, wr):
    S.op("dve", lambda e: e.tensor_scalar(out=wk_f[rows, :], in0=y_ps_ap, scalar1=shift, scalar2=None, op0=ALU.add), reads=rd, writes=["rr_f"])
    S.op("dve", lambda e: e.tensor_copy(out=wk_i[rows, :], in_=wk_f[rows, :]), reads=["rr_f"], writes=["rr_i"])
    S.op("dve", lambda e: e.tensor_copy(out=dst, in_=wk_i[rows, :]), reads=["rr_i"], writes=wr)
    S.op("dve", lambda e: e.tensor_tensor(out=dst, in0=wk_f[rows, :], in1=dst, op=ALU.subtract), reads=["rr_f"] + wr, writes=wr)
    S.op("act", lambda e: e.activation(out=dst, in_=dst, func=AF.Sin, scale=TWO_PI), reads=wr, writes=wr)


def stage_even(K, l):
    S, ps, dr, nc = K.S, K.ps, K.dr, K.nc
    S.barrier()
    A = K.A
    K.mT = _bf(A[:, 8192:16384]).rearrange("p (c t) -> p c t", c=8)
    K.op_base = 16384
    al = Alloc(A, 16384)
    QN = al.bf16(3 * L).rearrange("p (c t) -> p c t", c=3)
    KVN = al.bf16(2 * L).rearrange("p (c t) -> p c t", c=2)
    KR = al.bf16(L); KRS = al.bf16(L)
    tailB = al.p
    rsb = al.f32(512); sqt = al.f32(512)
    s5st = [al.bf16(512) for _ in range(2)]
    alm = Alloc(A, 8192, 16384)
    win = alm.bf16(8 * 1184).rearrange("p (k n) -> p k n", k=8)
    wkrs = alm.bf16(8 * 32).rearrange("p (k n) -> p k n", k=8)
    qlat = alm.f32(3 * 512).rearrange("p (c t) -> p c t", c=3)
    sqb = [alm.bf16(512) for _ in range(2)]
    gcol = alm.f32(8)
    qg, kvg = gcol[:, 0:3], gcol[:, 3:5]
    wd_ = dr["e_w_in"].rearrange("(k p) n -> p k n", p=128)
    for k in range(8):
        S.op("pool", lambda e, k=k: e.dma_start(out=win[:, k, :], in_=wd_[:, k, :]), writes=["win"], dma=True)
    S.op("sp", lambda e: e.dma_start(out=qg, in_=dr["qg_col"]), writes=["gcol"], dma=True)
    S.op("sp", lambda e: e.dma_start(out=kvg, in_=dr["kvg_col"]), writes=["gcol"], dma=True)
    S.op("act", lambda e: e.activation(out=wkrs[:, :, 0:16], in_=win[:, :, 656:672], func=AF.Copy, scale=-1.0), reads=["win"], writes=["wkrs"])
    S.op("act", lambda e: e.activation(out=wkrs[:, :, 16:32], in_=win[:, :, 640:656], func=AF.Copy), reads=["win"], writes=["wkrs"])
    ev = 0
    for tg in range(NG):
        ts_ = slice(512 * tg, 512 * tg + 512)
        for (nm, nchunk, c0, dst, gc) in (("q", 3, 0, QN, qg), ("kv", 2, 384, KVN, kvg)):
            for c in range(nchunk):
                pi = ev % 3
                ev += 1
                for k in range(8):
                    S.op("pe", lambda e, k=k, pi=pi, c=c, c0=c0, ts_=ts_: e.matmul(ps[pi][:, :], lhsT=win[:, k, c0 + 128 * c:c0 + 128 * c + 128], rhs=K.uT[:, k, ts_], start=(k == 0), stop=(k == 7)),
                         reads=["win", ("uT", k, tg)], writes=[("ps", pi)])
                S.op("act", lambda e, pi=pi, c=c: e.activation(out=qlat[:, c, :], in_=ps[pi][:, :], func=AF.Copy), reads=[("ps", pi)], writes=[("qlat", c)])
                sb_ = sqb[c % 2]
                S.op("act", lambda e, pi=pi, sb_=sb_: e.activation(out=sb_, in_=ps[pi][:, :], func=AF.Square), reads=[("ps", pi)], writes=[("sqb", c % 2)])
                S.op("pe", lambda e, sb_=sb_, c=c, nchunk=nchunk: e.matmul(ps[3][:, :], lhsT=K.onesb, rhs=sb_, start=(c == 0), stop=(c == nchunk - 1)),
                     reads=[("sqb", c % 2), "CB"], writes=[("ps", 3)])
            S.op("act", lambda e, nchunk=nchunk: e.activation(out=sqt, in_=ps[3][:, :], func=AF.Sqrt, scale=1.0 / (128 * nchunk), bias=EPS), reads=[("ps", 3)], writes=["sqt"])
            S.op("dve", lambda e: e.reciprocal(out=rsb, in_=sqt), reads=["sqt"], writes=["rsb"])
            for c in range(nchunk):
                S.op("dve", lambda e, c=c, dst=dst, gc=gc, ts_=ts_: e.scalar_tensor_tensor(out=dst[:, c, ts_], in0=qlat[:, c, :], scalar=gc[:, c:c + 1], in1=rsb, op0=ALU.mult, op1=ALU.mult),
                     reads=[("qlat", c), "rsb", "gcol"], writes=[(nm + "n", c)])
        for (wsrc, dstk, tok) in ((None, KR, "KR"), (wkrs, KRS, "KRS")):
            pi = 4 + (ev % 2)
            ev += 1
            for k in range(8):
                lhs = win[:, k, 640:672] if wsrc is None else wkrs[:, k, :]
                S.op("pe", lambda e, k=k, pi=pi, lhs=lhs, ts_=ts_: e.matmul(ps[pi][64:96, :], lhsT=lhs, rhs=K.uT[:, k, ts_], start=(k == 0), stop=(k == 7), tile_position=(0, 64)),
                     reads=["win", "wkrs", ("uT", k, tg)], writes=[("ps", pi)])
            S.op("act", lambda e, pi=pi, dstk=dstk, ts_=ts_: e.activation(out=dstk[64:96, ts_], in_=ps[pi][64:96, :], func=AF.Copy), reads=[("ps", pi)], writes=[tok])
        for c in range(4):
            pi = 6 + (ev % 2)
            ev += 1
            for k in range(8):
                S.op("pe", lambda e, k=k, pi=pi, c=c, ts_=ts_: e.matmul(ps[pi][:, :], lhsT=win[:, k, 672 + 128 * c:672 + 128 * c + 128], rhs=K.uT[:, k, ts_], start=(k == 0), stop=(k == 7)),
                     reads=["win", ("uT", k, tg)], writes=[("ps", pi)])
            sb2 = s5st[c % 2]
            S.op("act", lambda e, pi=pi, sb2=sb2: e.activation(out=sb2, in_=ps[pi][:, :], func=AF.Copy), reads=[("ps", pi)], writes=[("s5st", c % 2)])
            S.op("sp", lambda e, sb2=sb2, c=c, ts_=ts_: e.dma_start(out=K.s5u_d[c][:, ts_], in_=sb2), reads=[("s5st", c % 2)], writes=[("s5u_d", c)], dma=True)

    S.barrier()
    alA = Alloc(A, 0, 8192)
    Ct = alA.f32(L); St = alA.f32(L)
    WQ = alA.bf16(3 * 768).rearrange("p (c n) -> p c n", c=3)
    WQS = alA.bf16(3 * 8 * 32).rearrange("p (c h n) -> p c h n", c=3, h=8)
    WKV = alA.bf16(2 * 1024).rearrange("p (c n) -> p c n", c=2)
    QTb = [alA.bf16(L)]
    t1 = alA.f32(512)
    alB = Alloc(A, tailB)
    QTb.append(alB.bf16(L))
    KTb = [alB.bf16(L), alB.bf16(L)]
    Vb = [alB.bf16(NT * 128).rearrange("p (t n) -> p t n", t=NT) for _ in range(2)]
    Pb = [alB.bf16(512) for _ in range(2)]
    OTs = [alB.f32(512)]
    rcp = [alB.f32(512)]
    t2 = alB.f32(512)
    alC = Alloc(A, 12288, 16384)
    OTs.append(alC.f32(512)); rcp.append(alC.f32(512))
    wk_f = alC.f32(512); wk_i = alC.i32(512)
    posi = alC.i32(512); posf = alC.f32(512)
    rr = slice(64, 96)
    wq_d = dr["w_q_up"].rearrange("(c p) n -> p c n", p=128)
    wkv_d = dr["w_kv_up"].rearrange("(c p) n -> p c n", p=128)
    S.op("pool", lambda e: e.dma_start(out=WQ, in_=wq_d), writes=["WQ"], dma=True)
    S.op("pool", lambda e: e.dma_start(out=WKV, in_=wkv_d), writes=["WKV"], dma=True)
    WQv = WQ.rearrange("p c (h n) -> p c h n", h=8)
    S.op("act", lambda e: e.activation(out=WQS[:, :, :, 0:16], in_=WQv[:, :, :, 80:96], func=AF.Copy, scale=-1.0), reads=["WQ"], writes=["WQS"])
    S.op("act", lambda e: e.activation(out=WQS[:, :, :, 16:32], in_=WQv[:, :, :, 64:80], func=AF.Copy), reads=["WQ"], writes=["WQS"])
    for tg in range(NG):
        ts_ = slice(512 * tg, 512 * tg + 512)
        S.op("sp", lambda e, ts_=ts_: e.dma_start(out=posi[0:1, :], in_=dr["pos"][:, ts_]), writes=["posi"], dma=True)
        S.op("dve", lambda e: e.tensor_copy(out=posf[0:1, :], in_=posi[0:1, :]), reads=["posi"], writes=["posf"])
        S.op("pe", lambda e: e.matmul(ps[0][0:96, :], lhsT=K.invrow, rhs=posf[0:1, :], start=True, stop=True), reads=["posf", "CF"], writes=[("ps", 0)])
        range_reduce_sin(K, S, ps[0][rr, :], St[rr, ts_], 0.0, wk_i, wk_f, rr, [("ps", 0)], ["St"])
        range_reduce_sin(K, S, ps[0][rr, :], Ct[rr, ts_], 0.25, wk_i, wk_f, rr, [("ps", 0)], ["Ct"])
    for tg in range(NG):
        ts_ = slice(512 * tg, 512 * tg + 512)
        S.op("dve", lambda e, ts_=ts_: e.tensor_tensor(out=t1[rr, :], in0=KR[rr, ts_], in1=Ct[rr, ts_], op=ALU.mult), reads=["KR", "Ct"], writes=["t1"])
        S.op("dve", lambda e, ts_=ts_: e.tensor_tensor(out=t2[rr, :], in0=KRS[rr, ts_], in1=St[rr, ts_], op=ALU.mult), reads=["KRS", "St"], writes=["t2"])
        S.op("dve", lambda e, ts_=ts_: e.tensor_tensor(out=KR[rr, ts_], in0=t1[rr, :], in1=t2[rr, :], op=ALU.add), reads=["t1", "t2"], writes=["KR"])
    S.op("pool", lambda e: e.memset(Vb[0][:, :, 64:65], 1.0), writes=[("V", 0)])
    S.op("pool", lambda e: e.memset(Vb[1][:, :, 0:64], 0.0), writes=[("V", 1)])
    S.op("pool", lambda e: e.memset(Vb[1][:, :, 0:1], 1.0), writes=[("V", 1)])
    pr_i = 0
    for h in range(8):
        hb = h % 2
        QT, KT, Vh = QTb[hb], KTb[hb], Vb[hb]
        c_m, r0 = h // 2, 64 * (h % 2)
        orow = slice(r0, r0 + 64)
        for tg in range(NG):
            ts_ = slice(512 * tg, 512 * tg + 512)
            pq, pk = (0, 1) if tg % 2 == 0 else (2, 3)
            for c in range(3):
                S.op("pe", lambda e, c=c, pq=pq, h=h, ts_=ts_: e.matmul(ps[pq][0:96, :], lhsT=WQ[:, c, 96 * h:96 * h + 96], rhs=QN[:, c, ts_], start=(c == 0), stop=(c == 2)),
                     reads=["WQ", ("qn", c)], writes=[("ps", pq)])
            for c in range(3):
                S.op("pe", lambda e, c=c, pk=pk, h=h, ts_=ts_: e.matmul(ps[pk][64:96, :], lhsT=WQS[:, c, h, :], rhs=QN[:, c, ts_], start=(c == 0), stop=(c == 2), tile_position=(0, 64)),
                     reads=["WQS", ("qn", c)], writes=[("ps", pk)])
            S.op("act", lambda e, pq=pq, QT=QT, ts_=ts_: e.activation(out=QT[0:64, ts_], in_=ps[pq][0:64, :], func=AF.Copy), reads=[("ps", pq)], writes=[("QT", hb)])
            S.op("dve", lambda e, pq=pq, ts_=ts_: e.tensor_tensor(out=t1[rr, :], in0=ps[pq][rr, :], in1=Ct[rr, ts_], op=ALU.mult), reads=[("ps", pq), "Ct"], writes=["t1"])
            S.op("dve", lambda e, pk=pk, ts_=ts_: e.tensor_tensor(out=t2[rr, :], in0=ps[pk][rr, :], in1=St[rr, ts_], op=ALU.mult), reads=[("ps", pk), "St"], writes=["t2"])
            S.op("dve", lambda e, QT=QT, ts_=ts_: e.tensor_tensor(out=QT[rr, ts_], in0=t1[rr, :], in1=t2[rr, :], op=ALU.add), reads=["t1", "t2"], writes=[("QT", hb)])
            for c in range(2):
                S.op("pe", lambda e, c=c, h=h, ts_=ts_: e.matmul(ps[4][0:64, :], lhsT=WKV[:, c, 128 * h:128 * h + 64], rhs=KVN[:, c, ts_], start=(c == 0), stop=(c == 1)),
                     reads=["WKV", ("kvn", c)], writes=[("ps", 4)])
            S.op("act", lambda e, KT=KT, ts_=ts_: e.activation(out=KT[0:64, ts_], in_=ps[4][0:64, :], func=AF.Copy), reads=[("ps", 4)], writes=[("KT", hb)])
        S.op("pool", lambda e, KT=KT: e.tensor_copy(out=KT[rr, :], in_=KR[rr, :]), reads=["KR"], writes=[("KT", hb)])
        vcol = slice(0, 64) if hb == 0 else slice(64, 128)
        for t in range(NT):
            for c in range(2):
                S.op("pe", lambda e, c=c, t=t, h=h: e.matmul(ps[5][:, 0:64], lhsT=KVN[:, c, 128 * t:128 * t + 128], rhs=WKV[:, c, 128 * h + 64:128 * h + 128], start=(c == 0), stop=(c == 1)),
                     reads=["WKV", ("kvn", c)], writes=[("ps", 5)])
            S.op("act", lambda e, t=t, Vh=Vh, vcol=vcol: e.activation(out=Vh[:, t, vcol], in_=ps[5][:, 0:64], func=AF.Copy), reads=[("ps", 5)], writes=[("V", hb)])
        nrow = 65 if hb == 0 else 128
        for j in range(NG):
            po = 6 + (h * NG + j) % 2
            ob = (h * NG + j) % 2
            nkb = 4 * j + 4
            for kb in range(nkb):
                i = kb - 4 * j
                q0 = 128 * max(i, 0)
                qs = slice(q0, 512)
                b = pr_i % 2
                pr_i += 1
                pz = 2 * b
                S.op("pe", lambda e, pz=pz, KT=KT, QT=QT, kb=kb, j=j, q0=q0, qs=qs: e.matmul(ps[pz][:, qs], lhsT=KT[0:96, 128 * kb:128 * kb + 128], rhs=QT[0:96, 512 * j + q0:512 * j + 512], start=True, stop=True),
                     reads=[("QT", hb), ("KT", hb)], writes=[("ps", pz)])
                S.op("act", lambda e, b=b, pz=pz, qs=qs: e.activation(out=Pb[b][:, qs], in_=ps[pz][:, qs], func=AF.Exp, scale=MLA_SCALE), reads=[("ps", pz)], writes=[("P", b)])
                if i >= 0:
                    S.op("dve", lambda e, b=b, q0=q0: e.tensor_tensor(out=Pb[b][:, q0:q0 + 128], in0=Pb[b][:, q0:q0 + 128], in1=K.mle, op=ALU.mult), reads=[("P", b), "CB"], writes=[("P", b)])
                S.op("pe", lambda e, b=b, po=po, Vh=Vh, kb=kb, qs=qs, nkb=nkb, nrow=nrow: e.matmul(ps[po][0:nrow, qs], lhsT=Vh[:, kb, 0:nrow], rhs=Pb[b][:, qs], start=(kb == 0), stop=(kb == nkb - 1), skip_group_check=True),
                     reads=[("V", hb), ("P", b)], writes=[("ps", po)])
            S.op("act", lambda e, po=po, ob=ob, nrow=nrow: e.activation(out=OTs[ob][0:nrow, :], in_=ps[po][0:nrow, :], func=AF.Copy), reads=[("ps", po)], writes=[("OTs", ob)])
            sel = K.sel_e if hb == 0 else K.sel_o
            S.op("pe", lambda e, ob=ob, sel=sel, nrow=nrow: e.matmul(ps[5][:, :], lhsT=sel[0:nrow, :], rhs=OTs[ob][0:nrow, :], start=True, stop=True), reads=[("OTs", ob), "CF"], writes=[("ps", 5)])
            S.op("dve", lambda e, ob=ob, orow=orow: e.reciprocal(out=rcp[ob][orow, :], in_=ps[5][orow, :]), reads=[("ps", 5)], writes=[("rcp", ob)])
            dst = K.mT[orow, c_m, 512 * j:512 * j + 512]
            S.op("pool", lambda e, ob=ob, orow=orow, dst=dst: e.tensor_tensor(out=dst, in0=OTs[ob][orow, :], in1=rcp[ob][orow, :], op=ALU.mult), reads=[("OTs", ob), ("rcp", ob)], writes=[("mT", c_m)])
    if EVEN_MODE == "mla":
        S.barrier()
        for c in range(4, 8):
            S.op("pool", lambda e, c=c: e.memset(K.mT[:, c, :], 0.0), writes=[("mT", c)])
        return
    stage_s5(K, l)


def rr_sin(S, y_ap, dst, shift, wk_f, wk_i, rd, wr):
    S.op("dve", lambda e: e.tensor_scalar(out=wk_f, in0=y_ap, scalar1=shift, scalar2=None, op0=ALU.add), reads=rd, writes=["rr_f"])
    S.op("dve", lambda e: e.tensor_copy(out=wk_i, in_=wk_f), reads=["rr_f"], writes=["rr_i"])
    S.op("dve", lambda e: e.tensor_copy(out=dst, in_=wk_i), reads=["rr_i"], writes=wr)
    S.op("dve", lambda e: e.tensor_tensor(out=dst, in0=wk_f, in1=dst, op=ALU.subtract), reads=["rr_f"] + wr, writes=wr)
    S.op("act", lambda e: e.activation(out=dst, in_=dst, func=AF.Sin, scale=TWO_PI), reads=wr, writes=wr)


def stage_s5(K, l):
    S, ps, dr, nc = K.S, K.ps, K.dr, K.nc
    S.barrier()
    A = K.A
    alA = Alloc(A, 0, 8192)
    CtH = alA.f32(2048); StH = alA.f32(2048)
    BP = alA.bf16(16 * 2 * 128).rearrange("p (g v n) -> p g v n", g=16, v=2)
    CP = alA.bf16(16 * 2 * 128).rearrange("p (g v n) -> p g v n", g=16, v=2)
    alB = Alloc(A, 16384)
    S5U = alB.bf16(4 * L).rearrange("p (c t) -> p c t", c=4)
    v3 = lambda ap: ap.rearrange("p (a b) -> p a b", a=4)
    alC = Alloc(A, 12288, 16384)
    lre = alC.f32(256); lim = alC.f32(256); ldt = alC.f32(256); bre = alC.f32(256); bim = alC.f32(256)
    er = alC.f32(256); cs_ = alC.f32(256); sn_ = alC.f32(256); den = alC.f32(256)
    wr_ = alC.f32(256); wi_ = alC.f32(256); ta = alC.f32(256); tb = alC.f32(256)
    Bfr = alB.f32(256); Bfi = alB.f32(256)
    wkf = alC.f32(256); wki = alC.i32(256)
    thp = alB.f32(96).rearrange("p (g v) -> p g v", v=3)
    th = alB.f32(32); rcol = alB.f32(32); crT = alB.f32(32); srT = alB.f32(32); dt2 = alB.f32(32); lr2 = alB.f32(32); y128 = alB.f32(32)
    carry = alB.f32(32); ctmp = alB.f32(4); ctmp2 = alB.f32(4)
    c1raw = alB.f32(512).rearrange("p (g c) -> p g c", g=32); c2raw = alB.f32(512).rearrange("p (g c) -> p g c", g=32)
    DSK = alB.bf16(4 * 128).rearrange("p (c n) -> p c n", c=4)
    dcol = alB.f32(4); bglu = alB.f32(4)
    WGLU = alB.bf16(4 * 512).rearrange("p (c n) -> p c n", c=4)
    bt = [alB.f32(512) for _ in range(2)]
    t1 = [alB.f32(512) for _ in range(2)]
    xt = [alB.f32(512) for _ in range(2)]
    z1 = [alB.bf16(512) for _ in range(2)]
    z2 = [alB.bf16(512) for _ in range(2)]
    CF = K.identf
    rowmask = K.rowmask
    sign1 = K.sign1
    for c in range(4):
        S.op("sp", lambda e, c=c: e.dma_start(out=S5U[:, c, :], in_=K.s5u_d[c]), reads=[("s5u_d", c)], writes=[("s5u", c, j) for j in range(NT)], dma=True)
    for nm, ap in (("s5_lam_re", lre), ("s5_lam_im", lim), ("s5_logdt", ldt), ("s5_b_re", bre), ("s5_b_im", bim)):
        S.op("sp", lambda e, nm=nm, ap=ap: e.dma_start(out=ap, in_=dr[nm].rearrange("p a b -> p (a b)")), writes=["s5prm"], dma=True)
    S.op("sp", lambda e: e.dma_start(out=thp, in_=dr["s5_thp"]), writes=["thp"], dma=True)
    S.op("sp", lambda e: e.dma_start(out=c1raw, in_=dr["s5_c1"]), writes=["craw"], dma=True)
    S.op("sp", lambda e: e.dma_start(out=c2raw, in_=dr["s5_c2"]), writes=["craw"], dma=True)
    S.op("sp", lambda e: e.dma_start(out=dcol, in_=dr["s5_d_col"]), writes=["dcol"], dma=True)
    S.op("sp", lambda e: e.dma_start(out=bglu, in_=dr["s5_bglu_col"]), writes=["bglu"], dma=True)
    S.op("pool", lambda e: e.dma_start(out=WGLU, in_=dr["s5_w_glu"].rearrange("(c p) n -> p c n", p=128)), writes=["WGLU"], dma=True)
    D_ = lambda fn, rd, wr: S.op("dve", fn, reads=rd, writes=wr)
    P = ["s5prm"]
    D_(lambda e: e.tensor_scalar(out=lre, in0=lre, scalar1=-1e-4, scalar2=None, op0=ALU.min), P, P)
    S.op("act", lambda e: e.activation(out=ldt, in_=ldt, func=AF.Exp), reads=P, writes=P)
    D_(lambda e: e.tensor_tensor(out=ta, in0=lre, in1=ldt, op=ALU.mult), P, ["ta"])
    S.op("act", lambda e: e.activation(out=er, in_=ta, func=AF.Exp), reads=["ta"], writes=["er"])
    D_(lambda e: e.tensor_tensor(out=tb, in0=lim, in1=ldt, op=ALU.mult), P, ["tb"])
    D_(lambda e: e.tensor_scalar(out=tb, in0=tb, scalar1=1.0 / (2 * math.pi), scalar2=None, op0=ALU.mult), ["tb"], ["tb"])
    rr_sin(S, tb, sn_, 0.0, wkf, wki, ["tb"], ["sn_"])
    rr_sin(S, tb, cs_, 0.25, wkf, wki, ["tb"], ["cs_"])
    D_(lambda e: e.tensor_tensor(out=cs_, in0=cs_, in1=er, op=ALU.mult), ["cs_", "er"], ["cs_"])
    D_(lambda e: e.tensor_scalar(out=cs_, in0=cs_, scalar1=-1.0, scalar2=None, op0=ALU.add), ["cs_"], ["cs_"])
    D_(lambda e: e.tensor_tensor(out=sn_, in0=sn_, in1=er, op=ALU.mult), ["sn_", "er"], ["sn_"])
    D_(lambda e: e.tensor_tensor(out=den, in0=lre, in1=lre, op=ALU.mult), P, ["den"])
    D_(lambda e: e.tensor_tensor(out=ta, in0=lim, in1=lim, op=ALU.mult), P + ["er"], ["ta"])
    D_(lambda e: e.tensor_tensor(out=den, in0=den, in1=ta, op=ALU.add), ["den", "ta"], ["den"])
    D_(lambda e: e.reciprocal(out=den, in_=den), ["den"], ["den"])
    D_(lambda e: e.tensor_tensor(out=ta, in0=cs_, in1=lre, op=ALU.mult), ["cs_", "den"] + P, ["ta"])
    D_(lambda e: e.tensor_tensor(out=tb, in0=sn_, in1=lim, op=ALU.mult), ["sn_", "cs_"] + P, ["tb"])
    D_(lambda e: e.tensor_tensor(out=ta, in0=ta, in1=tb, op=ALU.add), ["ta", "tb"], ["ta"])
    D_(lambda e: e.tensor_tensor(out=wr_, in0=ta, in1=den, op=ALU.mult), ["ta", "den"], ["wr_"])
    D_(lambda e: e.tensor_tensor(out=ta, in0=sn_, in1=lre, op=ALU.mult), ["sn_", "wr_"] + P, ["ta"])
    D_(lambda e: e.tensor_tensor(out=tb, in0=cs_, in1=lim, op=ALU.mult), ["cs_", "wr_"] + P, ["tb"])
    D_(lambda e: e.tensor_tensor(out=ta, in0=ta, in1=tb, op=ALU.subtract), ["ta", "tb"], ["ta"])
    D_(lambda e: e.tensor_tensor(out=wi_, in0=ta, in1=den, op=ALU.mult), ["ta", "den"], ["wi_"])
    D_(lambda e: e.tensor_tensor(out=ta, in0=wr_, in1=bre, op=ALU.mult), ["wr_", "wi_"] + P, ["ta"])
    D_(lambda e: e.tensor_tensor(out=tb, in0=wi_, in1=bim, op=ALU.mult), ["wi_"] + P, ["tb"])
    D_(lambda e: e.tensor_tensor(out=Bfr, in0=ta, in1=tb, op=ALU.subtract), ["ta", "tb"], ["Bf"])
    D_(lambda e: e.tensor_tensor(out=ta, in0=wr_, in1=bim, op=ALU.mult), ["wr_", "Bf"] + P, ["ta"])
    D_(lambda e: e.tensor_tensor(out=tb, in0=wi_, in1=bre, op=ALU.mult), ["wi_", "Bf"] + P, ["tb"])
    D_(lambda e: e.tensor_tensor(out=Bfi, in0=ta, in1=tb, op=ALU.add), ["ta", "tb"], ["Bf"])
    T = ["thp"]
    D_(lambda e: e.tensor_scalar(out=lr2, in0=thp[:, :, 0], scalar1=-1e-4, scalar2=None, op0=ALU.min), T, ["lr2"])
    S.op("act", lambda e: e.activation(out=dt2, in_=thp[:, :, 2], func=AF.Exp), reads=T, writes=["dt2"])
    D_(lambda e: e.tensor_tensor(out=lr2, in0=lr2, in1=dt2, op=ALU.mult), ["lr2", "dt2"], ["lr2"])
    S.op("act", lambda e: e.activation(out=rcol, in_=lr2, func=AF.Exp), reads=["lr2"], writes=["rcol"])
    D_(lambda e: e.tensor_tensor(out=th, in0=thp[:, :, 1], in1=dt2, op=ALU.mult), T + ["dt2"], ["th"])
    D_(lambda e: e.tensor_scalar(out=th, in0=th, scalar1=1.0 / (2 * math.pi), scalar2=None, op0=ALU.mult), ["th"], ["th"])
    D_(lambda e: e.tensor_scalar(out=y128, in0=th, scalar1=128.0, scalar2=None, op0=ALU.mult), ["th"], ["y128"])
    rr_sin(S, y128, srT, 0.0, wkf[:, 0:32], wki[:, 0:32], ["y128"], ["srT"])
    rr_sin(S, y128, crT, 0.25, wkf[:, 0:32], wki[:, 0:32], ["y128"], ["crT"])
    for c in range(4):
        D_(lambda e, c=c: e.tensor_scalar(out=DSK[:, c, :], in0=K.identb, scalar1=dcol[:, c:c + 1], scalar2=None, op0=ALU.mult), ["dcol", "CB"], ["DSK"])
    Bfr3, Bfi3 = v3(Bfr), v3(Bfi)
    CtH3 = CtH.rearrange("p (g t) -> p g t", g=16); StH3 = StH.rearrange("p (g t) -> p g t", g=16)
    unit = 0
    for hf in range(2):
        for gq in range(16):
            g = 16 * hf + gq
            gh, gl = g // 8, g % 8
            rm = rowmask[:, gl:gl + 1]
            D_(lambda e, gq=gq, gh=gh, rm=rm: e.tensor_scalar(out=BP[:, gq, 0, 0:64], in0=Bfr3[:, gh, :], scalar1=rm, scalar2=None, op0=ALU.mult), ["Bf", "CF"], ["BP"])
            D_(lambda e, gq=gq, gh=gh, rm=rm: e.tensor_scalar(out=BP[:, gq, 0, 64:128], in0=Bfi3[:, gh, :], scalar1=rm, scalar2=None, op0=ALU.mult), ["Bf", "CF"], ["BP"])
            D_(lambda e, gq=gq, gh=gh, rm=rm: e.tensor_scalar(out=BP[:, gq, 1, 0:64], in0=Bfi3[:, gh, :], scalar1=rm, scalar2=None, op0=ALU.mult), ["Bf", "CF"], ["BP"])
            D_(lambda e, gq=gq, gh=gh, rm=rm: e.tensor_scalar(out=BP[:, gq, 1, 64:128], in0=Bfr3[:, gh, :], scalar1=rm, scalar2=-1.0, op0=ALU.mult, op1=ALU.mult), ["Bf", "CF"], ["BP"])
        S.op("pool", lambda e: e.memset(CP.rearrange("p g v n -> p (g v n)"), 0.0), writes=["CP"])
        for gq in range(16):
            g = 16 * hf + gq
            gl = g % 8
            D_(lambda e, gq=gq, g=g, gl=gl: e.tensor_scalar(out=CP[:, gq, 0, 16 * gl:16 * gl + 16], in0=c1raw[:, g, :], scalar1=sign1, scalar2=None, op0=ALU.mult), ["craw", "CF"], ["CP"])
            D_(lambda e, gq=gq, g=g, gl=gl: e.tensor_scalar(out=CP[:, gq, 1, 16 * gl:16 * gl + 16], in0=c2raw[:, g, :], scalar1=-1.0, scalar2=None, op0=ALU.mult), ["craw"], ["CP"])
            D_(lambda e, gq=gq, g=g: e.tensor_scalar(out=t1[0][:, 0:128], in0=K.iota, scalar1=th[:, g:g + 1], scalar2=None, op0=ALU.mult), ["th", "CF"], ["yturn"])
            rr_sin(S, t1[0][:, 0:128], StH3[:, gq, :], 0.0, wkf[:, 0:128], wki[:, 0:128], ["yturn"], ["tabs"])
            rr_sin(S, t1[0][:, 0:128], CtH3[:, gq, :], 0.25, wkf[:, 0:128], wki[:, 0:128], ["yturn"], ["tabs"])
        for j in range(NT):
            tsl = slice(128 * j, 128 * j + 128)
            for quad in range(4):
                c = 2 * hf + quad // 2
                b = unit % 2
                unit += 1
                pA, pB = (0, 1) if b == 0 else (2, 3)
                pY = 5 + (j * 2 + quad // 2) % 2
                qsl = slice(512 * quad, 512 * quad + 512)
                for i in range(4):
                    gq = 4 * quad + i
                    S.op("pe", lambda e, pA=pA, gq=gq, i=i, c=c, tsl=tsl: e.matmul(ps[pA][:, 128 * i:128 * i + 128], lhsT=BP[:, gq, 0, :], rhs=S5U[:, c, tsl], start=True, stop=True),
                         reads=["BP", ("s5u", c, j)], writes=[("ps", pA)])
                    S.op("pe", lambda e, pB=pB, gq=gq, i=i, c=c, tsl=tsl: e.matmul(ps[pB][:, 128 * i:128 * i + 128], lhsT=BP[:, gq, 1, :], rhs=S5U[:, c, tsl], start=True, stop=True),
                         reads=["BP", ("s5u", c, j)], writes=[("ps", pB)])
                D_(lambda e, b=b, pA=pA, qsl=qsl: e.tensor_tensor(out=t1[b], in0=ps[pA][:, :], in1=CtH[:, qsl], op=ALU.mult), [("ps", pA), "tabs"], [("t1", b)])
                D_(lambda e, b=b, pB=pB, qsl=qsl: e.tensor_tensor(out=bt[b], in0=ps[pB][:, :], in1=StH[:, qsl], op=ALU.mult), [("ps", pB), "tabs"], [("bt", b)])
                S.op("pool", lambda e, b=b: e.tensor_tensor(out=bt[b], in0=bt[b], in1=t1[b], op=ALU.add), reads=[("bt", b), ("t1", b)], writes=[("bt", b)])
                for i in range(4):
                    g = 16 * hf + 4 * quad + i
                    init = 0.0 if j == 0 else carry[:, g:g + 1]
                    D_(lambda e, b=b, i=i, g=g, init=init: e.tensor_tensor_scan(out=xt[b][:, 128 * i:128 * i + 128], data0=rcol[:, g:g + 1].to_broadcast([128, 128]),
                                                                             data1=bt[b][:, 128 * i:128 * i + 128], initial=init, op0=ALU.mult, op1=ALU.add),
                       [("bt", b), "rcol", ("carry", hf, quad)], [("xt", b)])
                if j < NT - 1:
                    g0 = 16 * hf + 4 * quad
                    xl = xt[b].rearrange("p (i t) -> p i t", i=4)[:, :, 127]
                    S.op("pe", lambda e, xl=xl: e.matmul(ps[4][:, 0:4], lhsT=K.swapf, rhs=xl, start=True, stop=True), reads=[("xt", b), "CF"], writes=[("ps", 4)])
                    D_(lambda e, xl=xl, g0=g0: e.tensor_tensor(out=ctmp, in0=xl, in1=crT[:, g0:g0 + 4], op=ALU.mult), [("xt", b), "crT"], ["ctmp"])
                    D_(lambda e, g0=g0: e.tensor_tensor(out=ctmp2, in0=ps[4][:, 0:4], in1=srT[:, g0:g0 + 4], op=ALU.mult), [("ps", 4), "srT"], ["ctmp2"])
                    D_(lambda e, g0=g0: e.tensor_tensor(out=carry[:, g0:g0 + 4], in0=ctmp, in1=ctmp2, op=ALU.add), ["ctmp", "ctmp2"], [("carry", hf, quad)])
                S.op("pool", lambda e, b=b, qsl=qsl: e.tensor_tensor(out=z1[b], in0=xt[b], in1=CtH[:, qsl], op=ALU.mult), reads=[("xt", b), "tabs"], writes=[("z1", b)])
                D_(lambda e, b=b, qsl=qsl: e.tensor_tensor(out=z2[b], in0=xt[b], in1=StH[:, qsl], op=ALU.mult), [("xt", b), "tabs"], [("z2", b)])
                first = (quad % 2 == 0)
                if first:
                    S.op("pe", lambda e, pY=pY, c=c, tsl=tsl: e.matmul(ps[pY][:, 0:128], lhsT=DSK[:, c, :], rhs=S5U[:, c, tsl], start=True, stop=False),
                         reads=["DSK", ("s5u", c, j)], writes=[("ps", pY)])
                for i in range(4):
                    gq = 4 * quad + i
                    last = (not first) and i == 3
                    S.op("pe", lambda e, pY=pY, gq=gq, b=b, i=i: e.matmul(ps[pY][:, 0:128], lhsT=CP[:, gq, 0, :], rhs=z1[b][:, 128 * i:128 * i + 128], start=False, stop=False),
                         reads=["CP", ("z1", b)], writes=[("ps", pY)])
                    S.op("pe", lambda e, pY=pY, gq=gq, b=b, i=i, last=last: e.matmul(ps[pY][:, 0:128], lhsT=CP[:, gq, 1, :], rhs=z2[b][:, 128 * i:128 * i + 128], start=False, stop=last),
                         reads=["CP", ("z2", b)], writes=[("ps", pY)])
                if not first:
                    S.op("act", lambda e, pY=pY, c=c, tsl=tsl: e.activation(out=S5U[:, c, tsl], in_=ps[pY][:, 0:128], func=GELU_TANH), reads=[("ps", pY)], writes=[("s5u", c, j)])
    sg = bt
    for tg in range(NG):
        ts_ = slice(512 * tg, 512 * tg + 512)
        for oc in range(4):
            pi = (tg * 4 + oc) % 4
            for kc in range(4):
                S.op("pe", lambda e, pi=pi, kc=kc, oc=oc, ts_=ts_: e.matmul(ps[pi][:, :], lhsT=WGLU[:, kc, 128 * oc:128 * oc + 128], rhs=S5U[:, kc, ts_], start=(kc == 0), stop=(kc == 3)),
                     reads=["WGLU"] + [("s5u", kc, jj) for jj in range(4 * tg, 4 * tg + 4)], writes=[("ps", pi)])
            sb_ = sg[oc % 2]
            S.op("act", lambda e, pi=pi, sb_=sb_, oc=oc: e.activation(out=sb_, in_=ps[pi][:, :], func=AF.Sigmoid, bias=bglu[:, oc:oc + 1]), reads=[("ps", pi), "bglu"], writes=[("bt", oc % 2)])
            D_(lambda e, sb_=sb_, oc=oc, ts_=ts_: e.tensor_tensor(out=K.mT[:, 4 + oc, ts_], in0=S5U[:, oc, ts_], in1=sb_, op=ALU.mult),
               [("bt", oc % 2)] + [("s5u", oc, jj) for jj in range(4 * tg, 4 * tg + 4)], [("mT", 4 + oc)])
    if EVEN_MODE == "s5":
        S.barrier()
        for c in range(4):
            S.op("pool", lambda e, c=c: e.memset(K.mT[:, c, :], 0.0), writes=[("mT", c)])
```

```python
import math
from contextlib import ExitStack
import numpy as np
import ml_dtypes
import concourse.bass as bass
import concourse.mybir as mybir
from concourse.bass_utils import run_bass_kernel_spmd

F32 = mybir.dt.float32
BF16 = mybir.dt.bfloat16
I32 = mybir.dt.int32
AF = mybir.ActivationFunctionType
ALU = mybir.AluOpType
AX = mybir.AxisListType


class Ev:
    __slots__ = ("key", "sem", "val")

    def __init__(self, key, sem, val):
        self.key, self.sem, self.val = key, sem, val


class Sched:
    ENGS = ("pe", "act", "dve", "pool", "sp")

    def __init__(self, nc, es):
        self.nc = nc
        self.q = {k: [] for k in self.ENGS}
        self.sem = {k: es.enter_context(nc.semaphore("s_" + k)) for k in ("pe", "act", "dve", "pool")}
        self.cnt = {k: 0 for k in self.sem}
        ns = {"sp": 12, "act": 4, "pool": 8}
        self.dsem = {k: [es.enter_context(nc.semaphore(f"d_{k}{i}")) for i in range(n)] for k, n in ns.items()}
        self.dcnt = {k: [0] * n for k, n in ns.items()}
        self.dnext = {k: 0 for k in ns}
        self.dlast = {k: [None] * n for k, n in ns.items()}
        self.lastw = {}
        self.readers = {}
        self.waited = {k: {} for k in self.ENGS}
        self.latest = {}
        self.pending = {k: [] for k in self.ENGS}
        self.nins = 0

    def op(self, eng, fn, reads=(), writes=(), dma=False):
        deps = []
        for t in reads:
            e = self.lastw.get(t)
            if e is not None:
                deps.append(e)
        for t in writes:
            e = self.lastw.get(t)
            if e is not None:
                deps.append(e)
            deps.extend(self.readers.get(t, ()))
        if self.pending[eng]:
            deps.extend(self.pending[eng])
            self.pending[eng] = []
        if dma:
            slot = self.dnext[eng]
            self.dnext[eng] = (slot + 1) % len(self.dsem[eng])
            if self.dlast[eng][slot] is not None:
                deps.append(self.dlast[eng][slot])
            self.dcnt[eng][slot] += 1
            ev = Ev(("d", eng, slot), self.dsem[eng][slot], 16 * self.dcnt[eng][slot])
            self.dlast[eng][slot] = ev
        else:
            self.cnt[eng] += 1
            ev = Ev((eng,), self.sem[eng], self.cnt[eng])
        waits = {}
        for d in deps:
            if d.key == (eng,) and eng == "pe":
                continue
            if self.waited[eng].get(d.key, 0) >= d.val:
                continue
            if d.key not in waits or waits[d.key].val < d.val:
                waits[d.key] = d
        for k, d in waits.items():
            self.waited[eng][k] = d.val
        self.q[eng].append((list(waits.values()), fn, ev, dma))
        self.latest[ev.key] = ev
        for t in reads:
            self.readers.setdefault(t, []).append(ev)
        for t in writes:
            self.lastw[t] = ev
            self.readers[t] = []
        self.nins += 1
        return ev

    def barrier(self):
        snap = list(self.latest.values())
        for k in self.ENGS:
            self.pending[k] = list(snap)

    def finish(self, eng="sp"):
        waits = []
        for k, d in self.latest.items():
            if self.waited[eng].get(k, 0) < d.val:
                waits.append(d)
        self.q[eng].append((waits, None, None, False))

    def emit(self, block):
        def mk(k):
            def f(e):
                for waits, fn, ev, dma in self.q[k]:
                    for w in waits:
                        e.wait_ge(w.sem, w.val)
                    if fn is None:
                        continue
                    ins = fn(e)
                    ins.then_inc(ev.sem, 16 if dma else 1)
            return f
        block.tensor(mk("pe"))
        block.scalar(mk("act"))
        block.vector(mk("dve"))
        block.gpsimd(mk("pool"))
        block.sync(mk("sp"))


L = 2048
D = 1024
NT = 16
NG = 4
FH = 2816
NHC = FH // 128
EPS = 1e-6
TWO_PI = 6.283185
ARENA_W = 30950
GELU_TANH = AF.Gelu_apprx_tanh
FFN_SILU = False


class Ctx:
    pass


def skew_emit(pairs):
    if not pairs:
        return
    pairs[0][0]()
    for p in range(len(pairs)):
        if p + 1 < len(pairs):
            pairs[p + 1][0]()
        pairs[p][1]()


def _bf(ap_f32):
    return ap_f32.bitcast(BF16)


class Alloc:
    def __init__(self, arena, base=0, limit=ARENA_W):
        self.a, self.p, self.limit = arena, base, limit

    def f32(self, n):
        ap = self.a[:, self.p:self.p + n]
        self.p += n
        assert self.p <= self.limit, ("arena overflow", self.p)
        return ap

    def bf16(self, n):
        w = (n + 1) // 2
        return _bf(self.f32(w))

    def i32(self, n):
        return self.f32(n).bitcast(I32)


def build_nc(plan=(("mix", 0), ("ffn", 0), ("mix", 1), ("ffn", 1)), dbg=None):
    nc = bass.Bass("TRN2", target_bir_lowering=False)
    K = Ctx()
    K.nc = nc
    dr = {}

    def din(name, shape, dt=F32):
        dr[name] = nc.dram_tensor(name, list(shape), dt, kind="ExternalInput").ap()
        return dr[name]

    din("x", [L, D]); din("ccol", [128, 8]); din("pos", [1, L], I32)
    din("mod_w", [2, D, 6 * D]); din("mod_b", [2, 1, 6 * D]); din("norm_g", [8, 1, D])
    din("w_out", [2, D, D]); din("w_gate", [2, D, FH]); din("w_up", [2, D, FH]); din("w_down", [2, FH, D])
    din("e_w_in", [D, 1184]); din("qg_col", [128, 3]); din("kvg_col", [128, 2])
    din("w_q_up", [384, 768]); din("w_kv_up", [256, 1024])
    din("s5_lam_re", [128, 4, 64]); din("s5_lam_im", [128, 4, 64]); din("s5_logdt", [128, 4, 64])
    din("s5_b_re", [128, 4, 64]); din("s5_b_im", [128, 4, 64])
    din("s5_thp", [128, 32, 3])
    din("s5_c1", [128, 32, 16]); din("s5_c2", [128, 32, 16])
    din("s5_d_col", [128, 4]); din("s5_w_glu", [512, 512]); din("s5_bglu_col", [128, 4])
    din("o_w_in", [D, 2560]); din("conv_w_col", [128, 4, 4]); din("conv_b_col", [128, 4])
    din("lru_wa", [4, 128, 128]); din("lru_wx", [4, 128, 128])
    din("lru_ba_col", [128, 4]); din("lru_bx_col", [128, 4]); din("lru_lam_col", [128, 4])
    din("cst_f32", [128, 1024]); din("cst_bf", [128, 1024], BF16)
    out_d = nc.dram_tensor("out", [L, D], F32, kind="ExternalOutput").ap()
    s5u_d = nc.dram_tensor("s5u_scr", [4, 128, L], BF16, kind="Internal").ap()
    K.dr, K.out_d, K.s5u_d = dr, out_d, s5u_d
    K.dbg = []
    K.dbgsel = dbg or ()

    with ExitStack() as es:
        Hh = es.enter_context(nc.sbuf_tensor("H", [128, NT * D], F32))
        CF = es.enter_context(nc.sbuf_tensor("CF", [128, 768], F32))
        CB = es.enter_context(nc.sbuf_tensor("CB", [128, 1024], BF16))
        MODS = es.enter_context(nc.sbuf_tensor("MODS", [128, 2 * 2048 + 64 + 64], F32))
        A = es.enter_context(nc.sbuf_tensor("ARENA", [128, ARENA_W], F32))
        ps = [es.enter_context(nc.psum_tensor(f"ps{i}", [128, 512], F32)) for i in range(8)]
        S = Sched(nc, es)
        block = es.enter_context(nc.Block())
        K.S, K.A, K.ps = S, A, ps
        K.H3 = Hh[:, :].rearrange("p (t f) -> p t f", t=NT)
        K.identf = CF[:, 0:128]; K.swapf = CF[:, 128:256]; K.iota = CF[:, 256:384]
        K.sel_e = CF[:, 384:512]; K.sel_o = CF[:, 512:640]; K.invrow = CF[0:1, 640:736]
        K.rowmask = CF[:, 736:744]; K.sign1 = CF[:, 744:745]
        K.identb = CB[:, 0:128]; K.onesb = CB[:, 128:256]; K.trib = CB[:, 256:384]
        K.mle = CB[:, 384:512]; K.mlt = CB[:, 512:640]
        K.gmix = [MODS[:, 0:1024], MODS[:, 2048:3072]]
        K.gffn = [MODS[:, 1024:2048], MODS[:, 3072:4096]]
        K.cols = [MODS[:, 4096:4128], MODS[:, 4128:4160]]
        K.small = MODS[:, 4160:4224]

        S.op("sp", lambda e: e.dma_start(out=CF[:, :], in_=dr["cst_f32"][:, 0:768]), writes=["CF"], dma=True)
        S.op("sp", lambda e: e.dma_start(out=CB[:, :], in_=dr["cst_bf"]), writes=["CB"], dma=True)
        for t in range(NT):
            S.op("sp", lambda e, t=t: e.dma_start(out=K.H3[:, t, :], in_=dr["x"][128 * t:128 * t + 128, :]),
                 writes=[("H", t)], dma=True)
        stage_mods(K)
        for kind, l in plan:
            S.barrier()
            if kind == "ffn":
                stage_prenorm(K, l, 1)
                stage_ffn(K, l)
            elif kind == "mix":
                stage_prenorm(K, l, 0)
                if l % 2 == 0:
                    stage_even(K, l)
                else:
                    stage_odd(K, l)
                stage_outproj(K, l)
        S.barrier()
        for t in range(NT):
            S.op("sp", lambda e, t=t: e.dma_start(out=out_d[128 * t:128 * t + 128, :], in_=K.H3[:, t, :]),
                 reads=[("H", t)], dma=True)
        if "mods" in K.dbgsel:
            K.dbg += [("cols0", K.cols[0]), ("cols1", K.cols[1]), ("gffn0", K.gffn[0]), ("gmix0", K.gmix[0])]
        if "uT" in K.dbgsel:
            K.dbg += [("uT%d" % c, K.uT[:, c, :]) for c in (0, 7)]
        for name, ap in K.dbg:
            dd = nc.dram_tensor("dbg_" + name, list(ap.shape), F32, kind="ExternalOutput").ap()
            S.op("pool", lambda e, dd=dd, ap=ap: e.dma_start(out=dd, in_=ap), dma=True)
        S.finish("sp")
        S.emit(block)
    return nc


def stage_mods(K):
    S, dr, ps = K.S, K.dr, K.ps
    al = Alloc(K.A)
    ccol = al.f32(8); cact = al.f32(8)
    cbc = al.f32(1024).rearrange("p (k m) -> p k m", k=8)
    wst = [al.f32(4096).rearrange("p (k n) -> p k n", k=8) for _ in range(2)]
    modb = al.f32(6144)
    ng = [al.f32(1024) for _ in range(4)]
    tmp = [al.f32(512) for _ in range(2)]
    junk = al.f32(128)
    S.op("sp", lambda e: e.dma_start(out=ccol, in_=dr["ccol"]), writes=["ccol"], dma=True)
    S.op("act", lambda e: e.activation(out=cact, in_=ccol, func=AF.Silu), reads=["ccol"], writes=["cact"])
    for k in range(8):
        S.op("dve", lambda e, k=k: e.tensor_copy(out=cbc[:, k, :], in_=cact[:, k:k + 1].to_broadcast([128, 128])),
             reads=["cact"], writes=["cbc"])
    it = 0
    for l in range(2):
        S.op("sp", lambda e, l=l: e.dma_start(out=modb, in_=dr["mod_b"][l].partition_broadcast(128)),
             writes=["modb"], dma=True)
        for n in range(4):
            S.op("sp", lambda e, l=l, n=n: e.dma_start(out=ng[n], in_=dr["norm_g"][4 * l + n].partition_broadcast(128)),
                 writes=[("ng", n)], dma=True)
        mw = dr["mod_w"][l].rearrange("(k p) n -> p k n", p=128)
        for grp in range(12):
            b = it % 2
            pb = ps[it % 2]
            it += 1
            for hk in range(2):
                S.op("sp" if hk == 0 else "act", lambda e, b=b, grp=grp, hk=hk, mw=mw: e.dma_start(
                    out=wst[b][:, 4 * hk:4 * hk + 4, :], in_=mw[:, 4 * hk:4 * hk + 4, 512 * grp:512 * grp + 512]),
                    writes=[("wst", b, hk)], dma=True)
            for k in range(8):
                S.op("pe", lambda e, b=b, k=k, pb=pb: e.matmul(pb[:, :], lhsT=cbc[:, k, :], rhs=wst[b][:, k, :],
                                                               start=(k == 0), stop=(k == 7)),
                     reads=["cbc", ("wst", b, k // 4)], writes=[("ps", b)])
            kind, half = grp // 2, grp % 2
            cs = slice(512 * half, 512 * half + 512)
            tb = tmp[b]
            S.op("dve", lambda e, pb=pb, tb=tb, grp=grp: e.tensor_tensor(out=tb, in0=pb[:, :], in1=modb[:, 512 * grp:512 * grp + 512], op=ALU.add),
                 reads=[("ps", b), "modb"], writes=[("mtmp", b)])
            if kind in (1, 4):
                g_ = ng[0 if kind == 1 else 2]
                S.op("dve", lambda e, tb=tb, g_=g_, cs=cs: e.scalar_tensor_tensor(out=tb, in0=tb, scalar=1.0, in1=g_[:, cs], op0=ALU.add, op1=ALU.mult),
                     reads=[("mtmp", b), ("ng", 0 if kind == 1 else 2)], writes=[("mtmp", b)])
            if kind in (0, 1, 3, 4):
                off = {1: 0, 0: 8, 4: 16, 3: 24}[kind]
                for q in range(4):
                    col = K.cols[l][:, off + 4 * half + q: off + 4 * half + q + 1]
                    S.op("dve", lambda e, tb=tb, q=q: e.tensor_tensor(out=junk, in0=tb[:, 128 * q:128 * q + 128], in1=K.identf, op=ALU.mult),
                         reads=[("mtmp", b), "CF"], writes=["mjunk"])
                    S.op("dve", lambda e, col=col: e.tensor_reduce(out=col, in_=junk, axis=AX.X, op=ALU.add),
                         reads=["mjunk"], writes=[("cols", l)])
            else:
                dst = (K.gmix if kind == 2 else K.gffn)[l]
                g_ = ng[1 if kind == 2 else 3]
                S.op("dve", lambda e, tb=tb, dst=dst, g_=g_, cs=cs: e.tensor_tensor(out=dst[:, cs], in0=tb, in1=g_[:, cs], op=ALU.mult),
                     reads=[("mtmp", b), ("ng", 1 if kind == 2 else 3)], writes=["gate"])


def stage_prenorm(K, l, which):
    S, ps = K.S, K.ps
    al = Alloc(K.A)
    K.uT = al.bf16(8 * L).rearrange("p (c t) -> p c t", c=8)
    K.after_uT = al.p
    xn = [al.bf16(1024) for _ in range(8)]
    junk = al.bf16(1024)
    ss = K.small[:, 0:16]; sq = K.small[:, 16:32]; rstd = K.small[:, 32:48]
    sc = K.cols[l][:, 16 * which: 16 * which + 8]
    sh = K.cols[l][:, 16 * which + 8: 16 * which + 16]
    for t in range(NT):
        S.op("act", lambda e, t=t: e.activation(out=junk, in_=K.H3[:, t, :], func=AF.Square, accum_out=ss[:, t:t + 1]),
             reads=[("H", t)], writes=["pn_junk", "pn_ss"])
    S.op("act", lambda e: e.activation(out=sq, in_=ss, func=AF.Sqrt, scale=1.0 / D, bias=EPS), reads=["pn_ss"], writes=["pn_sq"])
    S.op("dve", lambda e: e.reciprocal(out=rstd, in_=sq), reads=["pn_sq"], writes=["pn_rstd"])
    ev = 0
    for tg in range(NG):
        for i in range(4):
            t = 4 * tg + i
            b = t % 8
            S.op("act", lambda e, t=t, b=b: e.activation(out=xn[b], in_=K.H3[:, t, :], func=AF.Copy, scale=rstd[:, t:t + 1]),
                 reads=[("H", t), "pn_rstd"], writes=[("xn", b)])
        for c in range(8):
            pi = 2 + (ev % 2)
            pb = _bf(ps[pi][:, 0:256])
            for i in range(4):
                b = (4 * tg + i) % 8
                S.op("pe", lambda e, pb=pb, i=i, b=b, c=c: e.transpose(out=pb[:, 128 * i:128 * i + 128], in_=xn[b][:, 128 * c:128 * c + 128], identity=K.identb),
                     reads=[("xn", b), "CB"], writes=[("ps", pi)])
            dst = K.uT[:, c, 512 * tg:512 * tg + 512]
            if ev % 2 == 0:
                S.op("dve", lambda e, pb=pb, dst=dst, c=c: e.tensor_scalar(out=dst, in0=pb, scalar1=sc[:, c:c + 1], scalar2=sh[:, c:c + 1], op0=ALU.mult, op1=ALU.add),
                     reads=[("ps", pi), ("cols", l)], writes=[("uT", c, tg)])
            else:
                S.op("act", lambda e, pb=pb, dst=dst, c=c: e.activation(out=dst, in_=pb, func=AF.Identity, scale=sc[:, c:c + 1], bias=sh[:, c:c + 1]),
                     reads=[("ps", pi), ("cols", l)], writes=[("uT", c, tg)])
            ev += 1


def epilogue_tile(K, l, t, pA, pB, gate, tmp2):
    S, ps = K.S, K.ps
    ssA = K.small[:, 48:49]; ssB = K.small[:, 49:50]; s1 = K.small[:, 50:51]; s2 = K.small[:, 51:52]; r1 = K.small[:, 52:53]
    junk = tmp2[0]
    S.op("act", lambda e: e.activation(out=junk, in_=ps[pA][:, :], func=AF.Square, accum_out=ssA), reads=[("ps", pA)], writes=["ep_junk", "ep_ssA"])
    S.op("act", lambda e: e.activation(out=junk, in_=ps[pB][:, :], func=AF.Square, accum_out=ssB), reads=[("ps", pB)], writes=["ep_junk", "ep_ssB"])
    S.op("dve", lambda e: e.tensor_tensor(out=s1, in0=ssA, in1=ssB, op=ALU.add), reads=["ep_ssA", "ep_ssB"], writes=["ep_s1"])
    S.op("act", lambda e: e.activation(out=s2, in_=s1, func=AF.Sqrt, scale=1.0 / D, bias=EPS), reads=["ep_s1"], writes=["ep_s2"])
    S.op("dve", lambda e: e.reciprocal(out=r1, in_=s2), reads=["ep_s2"], writes=["ep_r1"])
    for n, pi in enumerate((pA, pB)):
        cs = slice(512 * n, 512 * n + 512)
        tm = tmp2[1 + n]
        S.op("dve", lambda e, pi=pi, tm=tm, cs=cs: e.scalar_tensor_tensor(out=tm, in0=ps[pi][:, :], scalar=r1, in1=gate[:, cs], op0=ALU.mult, op1=ALU.mult),
             reads=[("ps", pi), "ep_r1", "gate"], writes=[("ep_tm", n)])
        S.op("pool", lambda e, tm=tm, cs=cs, t=t: e.tensor_tensor(out=K.H3[:, t, cs], in0=K.H3[:, t, cs], in1=tm, op=ALU.add),
             reads=[("ep_tm", n), ("H", t)], writes=[("H", t)])


def stage_ffn(K, l, nhalf=4):
    S, ps, dr = K.S, K.ps, K.dr
    S.barrier()
    al = Alloc(K.A, K.after_uT)
    TH = L // nhalf
    hT = al.bf16(NHC * TH).rearrange("p (c t) -> p c t", c=NHC)
    wd = al.bf16(NHC * D).rearrange("p (c n) -> p c n", c=NHC)
    wg = [al.bf16(1024).rearrange("p (k m) -> p k m", k=8) for _ in range(2)]
    wu = [al.bf16(1024).rearrange("p (k m) -> p k m", k=8) for _ in range(2)]
    sil = [al.f32(512) for _ in range(2)]
    tmp2 = [al.f32(512) for _ in range(3)]
    wgd = dr["w_gate"][l].rearrange("(k p) n -> p k n", p=128)
    wud = dr["w_up"][l].rearrange("(k p) n -> p k n", p=128)
    wdd = dr["w_down"][l].rearrange("(c p) n -> p c n", p=128)
    for c in range(NHC):
        S.op("pool", lambda e, c=c: e.dma_start(out=wd[:, c, :], in_=wdd[:, c, :]), writes=[("wd", c)], dma=True)
    it = 0
    for hf in range(nhalf):
        for c in range(NHC):
            b = it % 2
            it += 1
            S.op("pool", lambda e, b=b, c=c: e.dma_start(out=wg[b], in_=wgd[:, :, 128 * c:128 * c + 128]), writes=[("wg", b)], dma=True)
            S.op("pool", lambda e, b=b, c=c: e.dma_start(out=wu[b], in_=wud[:, :, 128 * c:128 * c + 128]), writes=[("wu", b)], dma=True)
            for tg in range(TH // 512):
                tgg = hf * (TH // 512) + tg
                pg, pu = (0, 1) if (tg % 2 == 0) else (2, 3)
                for k in range(8):
                    S.op("pe", lambda e, b=b, k=k, pg=pg, tgg=tgg: e.matmul(ps[pg][:, :], lhsT=wg[b][:, k, :], rhs=K.uT[:, k, 512 * tgg:512 * tgg + 512], start=(k == 0), stop=(k == 7)),
                         reads=[("wg", b), ("uT", k, tgg)], writes=[("ps", pg)])
                for k in range(8):
                    S.op("pe", lambda e, b=b, k=k, pu=pu, tgg=tgg: e.matmul(ps[pu][:, :], lhsT=wu[b][:, k, :], rhs=K.uT[:, k, 512 * tgg:512 * tgg + 512], start=(k == 0), stop=(k == 7)),
                         reads=[("wu", b), ("uT", k, tgg)], writes=[("ps", pu)])
                sb = sil[tg % 2]
                if FFN_SILU:
                    S.op("act", lambda e, sb=sb, pg=pg: e.activation(out=sb, in_=ps[pg][:, :], func=AF.Silu), reads=[("ps", pg)], writes=[("sil", tg % 2)])
                else:
                    S.op("act", lambda e, sb=sb, pg=pg: e.activation(out=sb, in_=ps[pg][:, :], func=AF.Sigmoid), reads=[("ps", pg)], writes=[("sil", tg % 2)])
                    S.op("dve", lambda e, sb=sb, pg=pg: e.tensor_tensor(out=sb, in0=sb, in1=ps[pg][:, :], op=ALU.mult), reads=[("ps", pg), ("sil", tg % 2)], writes=[("sil", tg % 2)])
                S.op("dve", lambda e, sb=sb, pu=pu, c=c, tg=tg: e.tensor_tensor(out=hT[:, c, 512 * tg:512 * tg + 512], in0=sb, in1=ps[pu][:, :], op=ALU.mult),
                     reads=[("sil", tg % 2), ("ps", pu)], writes=[("hT", c)])
        for tt in range(TH // 128):
            t = hf * (TH // 128) + tt
            pA, pB = (4, 5) if (tt % 2 == 0) else (6, 7)
            for n, pi in enumerate((pA, pB)):
                for c in range(NHC):
                    S.op("pe", lambda e, c=c, tt=tt, n=n, pi=pi: e.matmul(ps[pi][:, :], lhsT=hT[:, c, 128 * tt:128 * tt + 128], rhs=wd[:, c, 512 * n:512 * n + 512], start=(c == 0), stop=(c == NHC - 1)),
                         reads=[("hT", c), ("wd", c)], writes=[("ps", pi)])
            epilogue_tile(K, l, t, pA, pB, K.gffn[l], tmp2)


def stage_outproj(K, l):
    S, ps, dr = K.S, K.ps, K.dr
    S.barrier()
    al = Alloc(K.A, K.op_base)
    wo = al.bf16(8 * D).rearrange("p (c n) -> p c n", c=8)
    tmp2 = [al.f32(512) for _ in range(3)]
    wod = dr["w_out"][l].rearrange("(c p) n -> p c n", p=128)
    for c in range(8):
        S.op("pool", lambda e, c=c: e.dma_start(out=wo[:, c, :], in_=wod[:, c, :]), writes=[("wo", c)], dma=True)
    for t in range(NT):
        pA, pB = (4, 5) if (t % 2 == 0) else (6, 7)
        for n, pi in enumerate((pA, pB)):
            for c in range(8):
                S.op("pe", lambda e, c=c, t=t, n=n, pi=pi: e.matmul(ps[pi][:, :], lhsT=K.mT[:, c, 128 * t:128 * t + 128], rhs=wo[:, c, 512 * n:512 * n + 512], start=(c == 0), stop=(c == 7)),
                     reads=[("mT", c), ("wo", c)], writes=[("ps", pi)])
        epilogue_tile(K, l, t, pA, pB, K.gmix[l], tmp2)


def _consts():
    cf = np.zeros((128, 1024), np.float32)
    cf[:, 0:128] = np.eye(128)
    sw = np.zeros((128, 128), np.float32)
    for m in range(64):
        sw[m + 64, m] = -1.0
        sw[m, m + 64] = 1.0
    cf[:, 128:256] = sw
    cf[:, 256:384] = np.arange(128, dtype=np.float32)[None, :]
    cf[64, 384:448] = 1.0
    cf[0, 512 + 64:512 + 128] = 1.0
    inv = (10000.0 ** (-np.arange(16, dtype=np.float64) / 16.0)) / (2 * np.pi)
    cf[0, 640 + 64:640 + 96] = np.concatenate([inv, inv]).astype(np.float32)
    for gl in range(8):
        cf[16 * gl:16 * gl + 16, 736 + gl] = 1.0
    cf[0:64, 744] = 1.0
    cf[64:128, 744] = -1.0
    cb = np.zeros((128, 1024), np.float32)
    cb[:, 0:128] = np.eye(128)
    cb[:, 128:256] = 1.0
    j = np.arange(128)[:, None]; s_ = np.arange(128)[None, :]
    cb[:, 256:384] = (j > s_)
    cb[:, 384:512] = (j <= s_)
    cb[:, 512:640] = (j < s_)
    return cf, cb.astype(ml_dtypes.bfloat16)


def _col(v, k):
    return np.ascontiguousarray(np.asarray(v, np.float32).reshape(k, 128).T)


def prep_core(inp, b):
    f = lambda a: np.ascontiguousarray(np.asarray(a, np.float32))
    cf, cb = _consts()
    m = {}
    m["x"] = f(inp["x"][b]); m["ccol"] = _col(inp["c"][b], 8)
    m["pos"] = np.ascontiguousarray(np.asarray(inp["positions"][b], np.int32).reshape(1, L))
    m["mod_w"] = f(inp["mod_w"]); m["mod_b"] = f(inp["mod_b"]).reshape(2, 1, 6 * D)
    m["norm_g"] = f(inp["norm_g"]).reshape(8, 1, D)
    m["w_out"] = f(inp["w_out"]); m["w_gate"] = f(inp["ffn_w_gate"]); m["w_up"] = f(inp["ffn_w_up"]); m["w_down"] = f(inp["ffn_w_down"])
    m["e_w_in"] = f(inp["even_w_in"][0]); m["qg_col"] = _col(inp["mla_q_norm_g"][0], 3); m["kvg_col"] = _col(inp["mla_kv_norm_g"][0], 2)
    m["w_q_up"] = f(inp["mla_w_q_up"][0]); m["w_kv_up"] = f(inp["mla_w_kv_up"][0])
    def gl_layout(a):
        a = np.asarray(a, np.float32).reshape(4, 8, 64).transpose(1, 0, 2)
        return np.ascontiguousarray(np.repeat(a[:, None], 16, axis=1).reshape(128, 4, 64))
    m["s5_lam_re"] = gl_layout(inp["s5_lam_re"][0]); m["s5_lam_im"] = gl_layout(inp["s5_lam_im"][0])
    m["s5_logdt"] = gl_layout(np.repeat(np.asarray(inp["s5_log_dt"][0], np.float32)[:, None], 64, axis=1))
    def b_layout(a):
        a = np.asarray(a, np.float32).reshape(4, 8, 64, 16).transpose(1, 3, 0, 2)
        return np.ascontiguousarray(a.reshape(128, 4, 64))
    m["s5_b_re"] = b_layout(inp["s5_b_re"][0]); m["s5_b_im"] = b_layout(inp["s5_b_im"][0])
    lr = np.asarray(inp["s5_lam_re"][0], np.float32).T; li = np.asarray(inp["s5_lam_im"][0], np.float32).T
    ld = np.repeat(np.asarray(inp["s5_log_dt"][0], np.float32)[None, :], 64, axis=0)
    thp = np.stack([np.concatenate([lr, lr], 0), np.concatenate([li, li], 0), np.concatenate([ld, ld], 0)], axis=-1)
    m["s5_thp"] = np.ascontiguousarray(thp.astype(np.float32))
    cr = np.asarray(inp["s5_c_re"][0], np.float32).transpose(2, 0, 1); ci = np.asarray(inp["s5_c_im"][0], np.float32).transpose(2, 0, 1)
    m["s5_c1"] = np.ascontiguousarray(np.concatenate([cr, ci], 0)); m["s5_c2"] = np.ascontiguousarray(np.concatenate([ci, cr], 0))
    m["s5_d_col"] = _col(np.asarray(inp["s5_d"][0]).reshape(512), 4); m["s5_w_glu"] = f(inp["s5_w_glu"][0]); m["s5_bglu_col"] = _col(inp["s5_b_glu"][0], 4)
    m["o_w_in"] = f(inp["odd_w_in"][0])
    cw = np.asarray(inp["lru_conv_w"][0], np.float32)
    m["conv_w_col"] = np.ascontiguousarray(cw.reshape(4, 4, 128).transpose(2, 1, 0))
    m["conv_b_col"] = _col(inp["lru_conv_b"][0], 4)
    def bd(w):
        w = np.asarray(w, np.float32); o = np.zeros((4, 128, 128), np.float32)
        for n in range(8):
            o[n // 2, 64 * (n % 2):64 * (n % 2) + 64, 64 * (n % 2):64 * (n % 2) + 64] = w[n]
        return o
    m["lru_wa"] = bd(inp["lru_w_a"][0]); m["lru_wx"] = bd(inp["lru_w_x"][0])
    m["lru_ba_col"] = _col(inp["lru_b_a"][0], 4); m["lru_bx_col"] = _col(inp["lru_b_x"][0], 4); m["lru_lam_col"] = _col(inp["lru_lambda"][0], 4)
    m["cst_f32"] = cf; m["cst_bf"] = cb
    return m


_NC_CACHE = {}
LAUNCH_PLANS = ((("mix", 0), ("ffn", 0), ("mix", 1), ("ffn", 1)),)


def kernel(**inputs):
    inputs = {k: np.asarray(v) for k, v in inputs.items()}
    in_maps = [prep_core(inputs, b) for b in range(8)]
    for plan in LAUNCH_PLANS:
        if plan not in _NC_CACHE:
            _NC_CACHE[plan] = build_nc(plan=plan)
        res = run_bass_kernel_spmd(_NC_CACHE[plan], in_maps, core_ids=list(range(8)))
        for b in range(8):
            in_maps[b]["x"] = np.ascontiguousarray(np.asarray(res.results[b]["out"], np.float32))
    out = np.stack([in_maps[b]["x"] for b in range(8)], axis=0)
    return out.astype(np.float32)


SB_SCALE = 0.125


def stage_odd(K, l):
    S, ps, dr, nc = K.S, K.ps, K.dr, K.nc
    S.barrier()
    A = K.A
    K.mT = _bf(A[:, 8192:16384]).rearrange("p (c t) -> p c t", c=8)
    K.op_base = 16384
    al = Alloc(A, 16384)
    QT = al.bf16(4 * L).rearrange("p (c t) -> p c t", c=4)
    KT = al.bf16(4 * L).rearrange("p (c t) -> p c t", c=4)
    V = al.bf16(NT * 512).rearrange("p (t f) -> p t f", t=NT)
    wfull = al.bf16(8 * 512)
    wst = [wfull[:, 1024 * i:1024 * i + 1024].rearrange("p (k m) -> p k m", k=8) for i in range(4)]
    wv = wfull.rearrange("p (k m) -> p k m", k=8)
    tail0 = al.p
    if not hasattr(K, "xr_d"):
        K.xr_d = nc.dram_tensor("xr_scr", [4, 128, L], F32, kind="Internal").ap()
        K.yg_d = nc.dram_tensor("yg_scr", [4, 128, L], F32, kind="Internal").ap()
    wd_ = dr["o_w_in"].rearrange("(k p) n -> p k n", p=128)
    stg = [A[:, 8192:8704]]
    it = 0
    chunks = [("q", c) for c in range(4)] + [("k", c) for c in range(4)] + [("xr", c) for c in range(4)] + [("yg", c) for c in range(4)]
    colbase = {"q": 0, "k": 512, "xr": 1536, "yg": 2048}
    evn = 0
    for kind, c in chunks:
        b = it % 4
        it += 1
        c0 = colbase[kind] + 128 * c
        S.op("pool", lambda e, b=b, c0=c0: e.dma_start(out=wst[b], in_=wd_[:, :, c0:c0 + 128]), writes=[("wst", b)], dma=True)
        for tg in range(NG):
            pi = evn % 4
            evn += 1
            for k in range(8):
                S.op("pe", lambda e, b=b, k=k, pi=pi, tg=tg: e.matmul(ps[pi][:, :], lhsT=wst[b][:, k, :], rhs=K.uT[:, k, 512 * tg:512 * tg + 512], start=(k == 0), stop=(k == 7)),
                     reads=[("wst", b), ("uT", k, tg)], writes=[("ps", pi)])
            if kind in ("q", "k"):
                dst = (QT if kind == "q" else KT)[:, c, 512 * tg:512 * tg + 512]
                eng = "act" if evn % 2 == 0 else "dve"
                if eng == "act":
                    S.op("act", lambda e, dst=dst, pi=pi: e.activation(out=dst, in_=ps[pi][:, :], func=AF.Copy), reads=[("ps", pi)], writes=[(kind, c)])
                else:
                    S.op("dve", lambda e, dst=dst, pi=pi: e.tensor_copy(out=dst, in_=ps[pi][:, :]), reads=[("ps", pi)], writes=[(kind, c)])
            else:
                S.op("act", lambda e, pi=pi: e.activation(out=stg[0], in_=ps[pi][:, :], func=AF.Copy), reads=[("ps", pi)], writes=["ostg"])
                dd = (K.xr_d if kind == "xr" else K.yg_d)[c][:, 512 * tg:512 * tg + 512]
                S.op("sp", lambda e, dd=dd: e.dma_start(out=dd, in_=stg[0]), reads=["ostg"], writes=[(kind + "_d", c)], dma=True)
    S.op("pool", lambda e: e.dma_start(out=wv, in_=wd_[:, :, 1024:1536]), writes=[("wst", 0), ("wst", 1), ("wst", 2), ("wst", 3)], dma=True)
    for t in range(NT):
        pi = 4 + t % 2
        for k in range(8):
            S.op("pe", lambda e, k=k, pi=pi, t=t: e.matmul(ps[pi][:, :], lhsT=K.uT[:, k, 128 * t:128 * t + 128], rhs=wv[:, k, :], start=(k == 0), stop=(k == 7)),
                 reads=[("wst", 0), ("uT", k, t // 4)], writes=[("ps", pi)])
        if t % 2 == 0:
            S.op("act", lambda e, pi=pi, t=t: e.activation(out=V[:, t, :], in_=ps[pi][:, :], func=AF.Copy), reads=[("ps", pi)], writes=["V"])
        else:
            S.op("dve", lambda e, pi=pi, t=t: e.tensor_copy(out=V[:, t, :], in_=ps[pi][:, :]), reads=[("ps", pi)], writes=["V"])
    S.barrier()
    al2 = Alloc(A, 0, 8192)
    B0 = al2.f32(2052); B1 = al2.f32(2048); B3 = al2.f32(2048)
    B4 = A[:, 8192:10240]
    al3 = Alloc(A, tail0 - 2048)
    B2 = _bf(A[:, 10240:11264])
    WA = al3.bf16(128); WX = al3.bf16(128)
    prm = al3.f32(64)
    cw = prm[:, 0:16].rearrange("p (c w) -> p c w", c=4); cb = prm[:, 16:20]; ba = prm[:, 20:24]; bx = prm[:, 24:28]
    lam = prm[:, 28:32]; colA = prm[:, 32:36]; ltmp = prm[:, 36:40]
    S.op("sp", lambda e: e.dma_start(out=prm[:, 0:16], in_=dr["conv_w_col"].rearrange("p c w -> p (c w)")), writes=["prm"], dma=True)
    for nm, ap in (("conv_b_col", cb), ("lru_ba_col", ba), ("lru_bx_col", bx), ("lru_lam_col", lam)):
        S.op("sp", lambda e, nm=nm, ap=ap: e.dma_start(out=ap, in_=dr[nm]), writes=["prm"], dma=True)
    S.op("act", lambda e: e.activation(out=ltmp, in_=lam, func=AF.Exp, scale=-1.0), reads=["prm"], writes=["ltmp"])
    S.op("act", lambda e: e.activation(out=ltmp, in_=ltmp, func=AF.Ln, bias=1.0), reads=["ltmp"], writes=["ltmp"])
    S.op("dve", lambda e: e.tensor_scalar(out=colA, in0=ltmp, scalar1=-8.0, scalar2=None, op0=ALU.mult), reads=["ltmp"], writes=["colA"])
    S.op("dve", lambda e: e.memset(B0[:, 0:3], 0.0), writes=["B0"])
    for c in range(4):
        S.op("pool", lambda e, c=c: e.dma_start(out=WA, in_=dr["lru_wa"][c]), writes=["WA"], dma=True)
        S.op("pool", lambda e, c=c: e.dma_start(out=WX, in_=dr["lru_wx"][c]), writes=["WX"], dma=True)
        S.op("sp", lambda e, c=c: e.dma_start(out=B0[:, 3:2051], in_=K.xr_d[c]), reads=[("xr_d", c)], writes=["B0"], dma=True)
        S.op("dve", lambda e, c=c: e.tensor_scalar(out=B1, in0=B0[:, 3:2051], scalar1=cw[:, c, 3:4], scalar2=cb[:, c:c + 1], op0=ALU.mult, op1=ALU.add),
             reads=["B0", "prm"], writes=["B1"])
        for w in range(3):
            S.op("dve", lambda e, c=c, w=w: e.scalar_tensor_tensor(out=B1, in0=B0[:, w:w + 2048], scalar=cw[:, c, w:w + 1], in1=B1, op0=ALU.mult, op1=ALU.add),
                 reads=["B0", "B1", "prm"], writes=["B1"])
        S.op("act", lambda e: e.activation(out=B2, in_=B1, func=AF.Copy), reads=["B1"], writes=["B2"])
        for tg in range(NG):
            cs = slice(512 * tg, 512 * tg + 512)
            pr, px = (0, 1) if tg % 2 == 0 else (2, 3)
            S.op("pe", lambda e, cs=cs, pr=pr: e.matmul(ps[pr][:, :], lhsT=WA, rhs=B2[:, cs], start=True, stop=True), reads=["WA", "B2"], writes=[("ps", pr)])
            S.op("pe", lambda e, cs=cs, px=px: e.matmul(ps[px][:, :], lhsT=WX, rhs=B2[:, cs], start=True, stop=True), reads=["WX", "B2"], writes=[("ps", px)])
            S.op("act", lambda e, cs=cs, pr=pr, c=c: e.activation(out=B3[:, cs], in_=ps[pr][:, :], func=AF.Sigmoid, bias=ba[:, c:c + 1]), reads=[("ps", pr), "prm"], writes=["B3"])
            S.op("act", lambda e, cs=cs, px=px, c=c: e.activation(out=B4[:, cs], in_=ps[px][:, :], func=AF.Sigmoid, bias=bx[:, c:c + 1]), reads=[("ps", px), "prm"], writes=["B4"])
        S.op("act", lambda e, c=c: e.activation(out=B3, in_=B3, func=AF.Exp, scale=colA[:, c:c + 1]), reads=["B3", "colA"], writes=["B3"])
        S.op("pool", lambda e: e.tensor_tensor(out=B4, in0=B4, in1=B1, op=ALU.mult), reads=["B4", "B1"], writes=["B4"])
        S.op("dve", lambda e: e.tensor_tensor(out=B1, in0=B3, in1=B3, op=ALU.mult), reads=["B3", "B4"], writes=["B1"])
        S.op("act", lambda e: e.activation(out=B1, in_=B1, func=AF.Sqrt, scale=-1.0, bias=1.0), reads=["B1"], writes=["B1"])
        S.op("dve", lambda e: e.tensor_tensor(out=B1, in0=B1, in1=B4, op=ALU.mult), reads=["B1", "B4"], writes=["B1"])
        S.op("dve", lambda e: e.tensor_tensor_scan(out=B4, data0=B3, data1=B1, initial=0.0, op0=ALU.mult, op1=ALU.add), reads=["B3", "B1"], writes=["B4"])
        S.op("sp", lambda e, c=c: e.dma_start(out=B0[:, 3:2051], in_=K.yg_d[c]), reads=[("yg_d", c), "B1"], writes=["B0"], dma=True)
        S.op("act", lambda e: e.activation(out=B0[:, 3:2051], in_=B0[:, 3:2051], func=GELU_TANH), reads=["B0"], writes=["B0"])
        S.op("dve", lambda e, c=c: e.tensor_tensor(out=K.mT[:, 4 + c, :], in0=B4, in1=B0[:, 3:2051], op=ALU.mult), reads=["B4", "B0"], writes=[("mT", 4 + c)])
    S.barrier()
    al4 = Alloc(A, 0, 8192)
    eb = [al4.f32(512) for _ in range(2)]; spb = [al4.f32(512) for _ in range(2)]
    tm = [al4.f32(512) for _ in range(2)]; tm2 = [al4.f32(512) for _ in range(2)]
    Ln_ = [al4.bf16(512) for _ in range(2)]; Wb = [al4.bf16(512) for _ in range(2)]
    Rb = [al4.f32(512) for _ in range(2)]
    def sb_pair(h, j, kb, b, first, last):
        c, r0 = h // 2, 64 * (h % 2)
        rows = slice(r0, r0 + 64)
        gi = (h * NG + j) % 2
        po = 6 + gi
        rb = Rb[gi]
        rbk = ("Rb", gi)
        nkb = 4 * j + 4
        i = kb - 4 * j
        q0 = 128 * max(i, 0)
        qs = slice(q0, 512)
        pz = b; psf = 2 + b; pcs = 4 + b

        def A():
            if first:
                S.op("pool", lambda e: e.memset(rb, 0.0), writes=[rbk])
            S.op("pe", lambda e: e.matmul(ps[pz][:, qs], lhsT=KT[rows, c, 128 * kb:128 * kb + 128], rhs=QT[rows, c, 512 * j + q0:512 * j + 512], start=True, stop=True),
                 reads=[("q", c), ("k", c)], writes=[("ps", pz)])
            S.op("act", lambda e: e.activation(out=eb[b][:, qs], in_=ps[pz][:, qs], func=AF.Exp, scale=-SB_SCALE), reads=[("ps", pz)], writes=[("eb", b)])
            S.op("act", lambda e: e.activation(out=spb[b][:, qs], in_=eb[b][:, qs], func=AF.Ln, bias=1.0), reads=[("eb", b)], writes=[("spb", b)])
            S.op("dve", lambda e: e.scalar_tensor_tensor(out=Ln_[b][:, qs], in0=ps[pz][:, qs], scalar=SB_SCALE, in1=spb[b][:, qs], op0=ALU.mult, op1=ALU.add),
                 reads=[("ps", pz), ("spb", b)], writes=[("Ln", b)])
            if i >= 0:
                S.op("dve", lambda e: e.tensor_tensor(out=Ln_[b][:, q0:q0 + 128], in0=Ln_[b][:, q0:q0 + 128], in1=K.mlt, op=ALU.mult),
                     reads=[("Ln", b), "CB"], writes=[("Ln", b)])
            S.op("pe", lambda e: e.matmul(ps[psf][:, qs], lhsT=K.trib, rhs=Ln_[b][:, qs], start=True, stop=True), reads=[("Ln", b), "CB"], writes=[("ps", psf)])
            S.op("pe", lambda e: e.matmul(ps[pcs][:, qs], lhsT=K.onesb, rhs=Ln_[b][:, qs], start=True, stop=True), reads=[("Ln", b), "CB"], writes=[("ps", pcs)])

        def B():
            S.op("dve", lambda e: e.tensor_tensor(out=tm[b][:, qs], in0=ps[psf][:, qs], in1=spb[b][:, qs], op=ALU.add),
                 reads=[("ps", psf), ("spb", b)], writes=[("tm", b)])
            S.op("pool", lambda e: e.tensor_tensor(out=tm2[b][:, qs], in0=tm[b][:, qs], in1=rb[:, qs], op=ALU.add),
                 reads=[("tm", b), rbk], writes=[("tm2", b)])
            S.op("act", lambda e: e.activation(out=Wb[b][:, qs], in_=tm2[b][:, qs], func=AF.Exp, scale=-1.0), reads=[("tm2", b)], writes=[("W", b)])
            if i >= 0:
                S.op("dve", lambda e: e.tensor_tensor(out=Wb[b][:, q0:q0 + 128], in0=Wb[b][:, q0:q0 + 128], in1=K.mlt, op=ALU.mult),
                     reads=[("W", b), "CB"], writes=[("W", b)])
            S.op("dve", lambda e: e.tensor_tensor(out=rb[:, qs], in0=rb[:, qs], in1=ps[pcs][:, qs], op=ALU.add),
                 reads=[("ps", pcs), rbk], writes=[rbk])
            S.op("pe", lambda e: e.matmul(ps[po][rows, qs], lhsT=V[:, kb, 64 * h:64 * h + 64], rhs=Wb[b][:, qs], start=(kb == nkb - 1), stop=(kb == 0), tile_position=(0, r0), skip_group_check=True),
                 reads=["V", ("W", b)], writes=[("ps", po)])
            if last:
                dst = K.mT[rows, c, 512 * j:512 * j + 512]
                S.op("act", lambda e: e.activation(out=dst, in_=ps[po][rows, :], func=AF.Copy), reads=[("ps", po)], writes=[("mT", c)])
        return A, B

    pairs = []
    for h in range(8):
        for j in range(NG):
            nkb = 4 * j + 4
            for kb in range(nkb - 1, -1, -1):
                pairs.append(sb_pair(h, j, kb, len(pairs) % 2, kb == nkb - 1, kb == 0))
    for A_, B_ in pairs:
        A_()
        B_()

MLA_SCALE = 96 ** -0.5
EVEN_MODE = "full"


def range_reduce_sin(K, S, y_ps_ap, dst, shift, wk_i, wk_f, rows, rd, wr):
    S.op("dve", lambda e: e.tensor_scalar(out=wk_f[rows, :], in0=y_ps_ap, scalar1=shift, scalar2=None, op0=ALU.add), reads=rd, writes=["rr_f"])
    S.op("dve", lambda e: e.tensor_copy(out=wk_i[rows, :], in_=wk_f[rows, :]), reads=["rr_f"], writes=["rr_i"])
    S.op("dve", lambda e: e.tensor_copy(out=dst, in_=wk_i[rows, :]), reads=["rr_i"], writes=wr)
    S.op("dve", lambda e: e.tensor_tensor(out=dst, in0=wk_f[rows, :], in1=dst, op=ALU.subtract), reads=["rr_f"] + wr, writes=wr)
    S.op("act", lambda e: e.activation(out=dst, in_=dst, func=AF.Sin, scale=TWO_PI), reads=wr, writes=wr)


def mla_pair(K, S, ps, h, j, kb, b, QT, KT, Vh, Pb, OTs, rcp, nrow, last):
    hb = h % 2
    c_m, r0 = h // 2, 64 * (h % 2)
    orow = slice(r0, r0 + 64)
    po = 6 + (h * NG + j) % 2
    ob = (h * NG + j) % 2
    nkb = 4 * j + 4
    i = kb - 4 * j
    q0 = 128 * max(i, 0)
    qs = slice(q0, 512)
    pz = 2 * b

    def A():
        S.op("pe", lambda e: e.matmul(ps[pz][:, qs], lhsT=KT[0:96, 128 * kb:128 * kb + 128], rhs=QT[0:96, 512 * j + q0:512 * j + 512], start=True, stop=True),
             reads=[("QT", hb), ("KT", hb)], writes=[("ps", pz)])
        S.op("act", lambda e: e.activation(out=Pb[b][:, qs], in_=ps[pz][:, qs], func=AF.Exp, scale=MLA_SCALE), reads=[("ps", pz)], writes=[("P", b)])
        if i >= 0:
            S.op("dve", lambda e: e.tensor_tensor(out=Pb[b][:, q0:q0 + 128], in0=Pb[b][:, q0:q0 + 128], in1=K.mle, op=ALU.mult), reads=[("P", b), "CB"], writes=[("P", b)])

    def B():
        S.op("pe", lambda e: e.matmul(ps[po][0:nrow, qs], lhsT=Vh[:, kb, 0:nrow], rhs=Pb[b][:, qs], start=(kb == 0), stop=(kb == nkb - 1), skip_group_check=True),
             reads=[("V", hb), ("P", b)], writes=[("ps", po)])
        if last:
            S.op("act", lambda e: e.activation(out=OTs[ob][0:nrow, :], in_=ps[po][0:nrow, :], func=AF.Copy), reads=[("ps", po)], writes=[("OTs", ob)])
            sel = K.sel_e if hb == 0 else K.sel_o
            S.op("pe", lambda e: e.matmul(ps[5][:, :], lhsT=sel[0:nrow, :], rhs=OTs[ob][0:nrow, :], start=True, stop=True), reads=[("OTs", ob), "CF"], writes=[("ps", 5)])
            S.op("dve", lambda e: e.reciprocal(out=rcp[ob][orow, :], in_=ps[5][orow, :]), reads=[("ps", 5)], writes=[("rcp", ob)])
            dst = K.mT[orow, c_m, 512 * j:512 * j + 512]
            S.op("pool", lambda e: e.tensor_tensor(out=dst, in0=OTs[ob][orow, :], in1=rcp[ob][orow, :], op=ALU.mult), reads=[("OTs", ob), ("rcp", ob)], writes=[("mT", c_m)])
    return A, B


def stage_even(K, l):
    S, ps, dr, nc = K.S, K.ps, K.dr, K.nc
    S.barrier()
    A = K.A
    K.mT = _bf(A[:, 8192:16384]).rearrange("p (c t) -> p c t", c=8)
    K.op_base = 16384
    al = Alloc(A, 16384)
    QN = al.bf16(3 * L).rearrange("p (c t) -> p c t", c=3)
    KVN = al.bf16(2 * L).rearrange("p (c t) -> p c t", c=2)
    KR = al.bf16(L); KRS = al.bf16(L)
    tailB = al.p
    rsb = al.f32(512); sqt = al.f32(512)
    s5st = [al.bf16(512) for _ in range(2)]
    alm = Alloc(A, 8192, 16384)
    win = alm.bf16(8 * 1184).rearrange("p (k n) -> p k n", k=8)
    wkrs = alm.bf16(8 * 32).rearrange("p (k n) -> p k n", k=8)
    qlat = alm.f32(3 * 512).rearrange("p (c t) -> p c t", c=3)
    sqb = [alm.bf16(512) for _ in range(2)]
    gcol = alm.f32(8)
    qg, kvg = gcol[:, 0:3], gcol[:, 3:5]
    wd_ = dr["e_w_in"].rearrange("(k p) n -> p k n", p=128)
    for k in range(8):
        S.op("pool", lambda e, k=k: e.dma_start(out=win[:, k, :], in_=wd_[:, k, :]), writes=["win"], dma=True)
    S.op("sp", lambda e: e.dma_start(out=qg, in_=dr["qg_col"]), writes=["gcol"], dma=True)
    S.op("sp", lambda e: e.dma_start(out=kvg, in_=dr["kvg_col"]), writes=["gcol"], dma=True)
    S.op("act", lambda e: e.activation(out=wkrs[:, :, 0:16], in_=win[:, :, 656:672], func=AF.Copy, scale=-1.0), reads=["win"], writes=["wkrs"])
    S.op("act", lambda e: e.activation(out=wkrs[:, :, 16:32], in_=win[:, :, 640:656], func=AF.Copy), reads=["win"], writes=["wkrs"])
    ev = 0
    for tg in range(NG):
        ts_ = slice(512 * tg, 512 * tg + 512)
        for (nm, nchunk, c0, dst, gc) in (("q", 3, 0, QN, qg), ("kv", 2, 384, KVN, kvg)):
            for c in range(nchunk):
                pi = ev % 3
                ev += 1
                for k in range(8):
                    S.op("pe", lambda e, k=k, pi=pi, c=c, c0=c0, ts_=ts_: e.matmul(ps[pi][:, :], lhsT=win[:, k, c0 + 128 * c:c0 + 128 * c + 128], rhs=K.uT[:, k, ts_], start=(k == 0), stop=(k == 7)),
                         reads=["win", ("uT", k, tg)], writes=[("ps", pi)])
                S.op("act", lambda e, pi=pi, c=c: e.activation(out=qlat[:, c, :], in_=ps[pi][:, :], func=AF.Copy), reads=[("ps", pi)], writes=[("qlat", c)])
                sb_ = sqb[c % 2]
                S.op("act", lambda e, pi=pi, sb_=sb_: e.activation(out=sb_, in_=ps[pi][:, :], func=AF.Square), reads=[("ps", pi)], writes=[("sqb", c % 2)])
                S.op("pe", lambda e, sb_=sb_, c=c, nchunk=nchunk: e.matmul(ps[3][:, :], lhsT=K.onesb, rhs=sb_, start=(c == 0), stop=(c == nchunk - 1)),
                     reads=[("sqb", c % 2), "CB"], writes=[("ps", 3)])
            S.op("act", lambda e, nchunk=nchunk: e.activation(out=sqt, in_=ps[3][:, :], func=AF.Sqrt, scale=1.0 / (128 * nchunk), bias=EPS), reads=[("ps", 3)], writes=["sqt"])
            S.op("dve", lambda e: e.reciprocal(out=rsb, in_=sqt), reads=["sqt"], writes=["rsb"])
            for c in range(nchunk):
                S.op("dve", lambda e, c=c, dst=dst, gc=gc, ts_=ts_: e.scalar_tensor_tensor(out=dst[:, c, ts_], in0=qlat[:, c, :], scalar=gc[:, c:c + 1], in1=rsb, op0=ALU.mult, op1=ALU.mult),
                     reads=[("qlat", c), "rsb", "gcol"], writes=[(nm + "n", c)])
        for (wsrc, dstk, tok) in ((None, KR, "KR"), (wkrs, KRS, "KRS")):
            pi = 4 + (ev % 2)
            ev += 1
            for k in range(8):
                lhs = win[:, k, 640:672] if wsrc is None else wkrs[:, k, :]
                S.op("pe", lambda e, k=k, pi=pi, lhs=lhs, ts_=ts_: e.matmul(ps[pi][64:96, :], lhsT=lhs, rhs=K.uT[:, k, ts_], start=(k == 0), stop=(k == 7), tile_position=(0, 64)),
                     reads=["win", "wkrs", ("uT", k, tg)], writes=[("ps", pi)])
            S.op("act", lambda e, pi=pi, dstk=dstk, ts_=ts_: e.activation(out=dstk[64:96, ts_], in_=ps[pi][64:96, :], func=AF.Copy), reads=[("ps", pi)], writes=[tok])
        for c in range(4):
            pi = 6 + (ev % 2)
            ev += 1
            for k in range(8):
                S.op("pe", lambda e, k=k, pi=pi, c=c, ts_=ts_: e.matmul(ps[pi][:, :], lhsT=win[:, k, 672 + 128 * c:672 + 128 * c + 128], rhs=K.uT[:, k, ts_], start=(k == 0), stop=(k == 7)),
                     reads=["win", ("uT", k, tg)], writes=[("ps", pi)])
            sb2 = s5st[c % 2]
            S.op("act", lambda e, pi=pi, sb2=sb2: e.activation(out=sb2, in_=ps[pi][:, :], func=AF.Copy), reads=[("ps", pi)], writes=[("s5st", c % 2)])
            S.op("sp", lambda e, sb2=sb2, c=c, ts_=ts_: e.dma_start(out=K.s5u_d[c][:, ts_], in_=sb2), reads=[("s5st", c % 2)], writes=[("s5u_d", c)], dma=True)

    S.barrier()
    alA = Alloc(A, 0, 8192)
    Ct = alA.f32(L); St = alA.f32(L)
    WQ = alA.bf16(3 * 768).rearrange("p (c n) -> p c n", c=3)
    WQS = alA.bf16(3 * 8 * 32).rearrange("p (c h n) -> p c h n", c=3, h=8)
    WKV = alA.bf16(2 * 1024).rearrange("p (c n) -> p c n", c=2)
    QTb = [alA.bf16(L)]
    t1 = alA.f32(512)
    alB = Alloc(A, tailB)
    QTb.append(alB.bf16(L))
    KTb = [alB.bf16(L), alB.bf16(L)]
    Vb = [alB.bf16(NT * 128).rearrange("p (t n) -> p t n", t=NT) for _ in range(2)]
    Pb = [alB.bf16(512) for _ in range(2)]
    OTs = [alB.f32(512)]
    rcp = [alB.f32(512)]
    t2 = alB.f32(512)
    alC = Alloc(A, 12288, 16384)
    OTs.append(alC.f32(512)); rcp.append(alC.f32(512))
    wk_f = alC.f32(512); wk_i = alC.i32(512)
    posi = alC.i32(512); posf = alC.f32(512)
    rr = slice(64, 96)
    wq_d = dr["w_q_up"].rearrange("(c p) n -> p c n", p=128)
    wkv_d = dr["w_kv_up"].rearrange("(c p) n -> p c n", p=128)
    S.op("pool", lambda e: e.dma_start(out=WQ, in_=wq_d), writes=["WQ"], dma=True)
    S.op("pool", lambda e: e.dma_start(out=WKV, in_=wkv_d), writes=["WKV"], dma=True)
    WQv = WQ.rearrange("p c (h n) -> p c h n", h=8)
    S.op("act", lambda e: e.activation(out=WQS[:, :, :, 0:16], in_=WQv[:, :, :, 80:96], func=AF.Copy, scale=-1.0), reads=["WQ"], writes=["WQS"])
    S.op("act", lambda e: e.activation(out=WQS[:, :, :, 16:32], in_=WQv[:, :, :, 64:80], func=AF.Copy), reads=["WQ"], writes=["WQS"])
    for tg in range(NG):
        ts_ = slice(512 * tg, 512 * tg + 512)
        S.op("sp", lambda e, ts_=ts_: e.dma_start(out=posi[0:1, :], in_=dr["pos"][:, ts_]), writes=["posi"], dma=True)
        S.op("dve", lambda e: e.tensor_copy(out=posf[0:1, :], in_=posi[0:1, :]), reads=["posi"], writes=["posf"])
        S.op("pe", lambda e: e.matmul(ps[0][0:96, :], lhsT=K.invrow, rhs=posf[0:1, :], start=True, stop=True), reads=["posf", "CF"], writes=[("ps", 0)])
        range_reduce_sin(K, S, ps[0][rr, :], St[rr, ts_], 0.0, wk_i, wk_f, rr, [("ps", 0)], ["St"])
        range_reduce_sin(K, S, ps[0][rr, :], Ct[rr, ts_], 0.25, wk_i, wk_f, rr, [("ps", 0)], ["Ct"])
    for tg in range(NG):
        ts_ = slice(512 * tg, 512 * tg + 512)
        S.op("dve", lambda e, ts_=ts_: e.tensor_tensor(out=t1[rr, :], in0=KR[rr, ts_], in1=Ct[rr, ts_], op=ALU.mult), reads=["KR", "Ct"], writes=["t1"])
        S.op("dve", lambda e, ts_=ts_: e.tensor_tensor(out=t2[rr, :], in0=KRS[rr, ts_], in1=St[rr, ts_], op=ALU.mult), reads=["KRS", "St"], writes=["t2"])
        S.op("dve", lambda e, ts_=ts_: e.tensor_tensor(out=KR[rr, ts_], in0=t1[rr, :], in1=t2[rr, :], op=ALU.add), reads=["t1", "t2"], writes=["KR"])
    S.op("pool", lambda e: e.memset(Vb[0][:, :, 64:65], 1.0), writes=[("V", 0)])
    S.op("pool", lambda e: e.memset(Vb[1][:, :, 0:64], 0.0), writes=[("V", 1)])
    S.op("pool", lambda e: e.memset(Vb[1][:, :, 0:1], 1.0), writes=[("V", 1)])
    pr_i = 0
    for h in range(8):
        hb = h % 2
        QT, KT, Vh = QTb[hb], KTb[hb], Vb[hb]
        c_m, r0 = h // 2, 64 * (h % 2)
        orow = slice(r0, r0 + 64)
        for tg in range(NG):
            ts_ = slice(512 * tg, 512 * tg + 512)
            pq, pk = (0, 1) if tg % 2 == 0 else (2, 3)
            for c in range(3):
                S.op("pe", lambda e, c=c, pq=pq, h=h, ts_=ts_: e.matmul(ps[pq][0:96, :], lhsT=WQ[:, c, 96 * h:96 * h + 96], rhs=QN[:, c, ts_], start=(c == 0), stop=(c == 2)),
                     reads=["WQ", ("qn", c)], writes=[("ps", pq)])
            for c in range(3):
                S.op("pe", lambda e, c=c, pk=pk, h=h, ts_=ts_: e.matmul(ps[pk][64:96, :], lhsT=WQS[:, c, h, :], rhs=QN[:, c, ts_], start=(c == 0), stop=(c == 2), tile_position=(0, 64)),
                     reads=["WQS", ("qn", c)], writes=[("ps", pk)])
            S.op("act", lambda e, pq=pq, QT=QT, ts_=ts_: e.activation(out=QT[0:64, ts_], in_=ps[pq][0:64, :], func=AF.Copy), reads=[("ps", pq)], writes=[("QT", hb)])
            S.op("dve", lambda e, pq=pq, ts_=ts_: e.tensor_tensor(out=t1[rr, :], in0=ps[pq][rr, :], in1=Ct[rr, ts_], op=ALU.mult), reads=[("ps", pq), "Ct"], writes=["t1"])
            S.op("dve", lambda e, pk=pk, ts_=ts_: e.tensor_tensor(out=t2[rr, :], in0=ps[pk][rr, :], in1=St[rr, ts_], op=ALU.mult), reads=[("ps", pk), "St"], writes=["t2"])
            S.op("dve", lambda e, QT=QT, ts_=ts_: e.tensor_tensor(out=QT[rr, ts_], in0=t1[rr, :], in1=t2[rr, :], op=ALU.add), reads=["t1", "t2"], writes=[("QT", hb)])
            for c in range(2):
                S.op("pe", lambda e, c=c, h=h, ts_=ts_: e.matmul(ps[4][0:64, :], lhsT=WKV[:, c, 128 * h:128 * h + 64], rhs=KVN[:, c, ts_], start=(c == 0), stop=(c == 1)),
                     reads=["WKV", ("kvn", c)], writes=[("ps", 4)])
            S.op("act", lambda e, KT=KT, ts_=ts_: e.activation(out=KT[0:64, ts_], in_=ps[4][0:64, :], func=AF.Copy), reads=[("ps", 4)], writes=[("KT", hb)])
        S.op("pool", lambda e, KT=KT: e.tensor_copy(out=KT[rr, :], in_=KR[rr, :]), reads=["KR"], writes=[("KT", hb)])
        vcol = slice(0, 64) if hb == 0 else slice(64, 128)
        for t in range(NT):
            for c in range(2):
                S.op("pe", lambda e, c=c, t=t, h=h: e.matmul(ps[5][:, 0:64], lhsT=KVN[:, c, 128 * t:128 * t + 128], rhs=WKV[:, c, 128 * h + 64:128 * h + 128], start=(c == 0), stop=(c == 1)),
                     reads=["WKV", ("kvn", c)], writes=[("ps", 5)])
            S.op("act", lambda e, t=t, Vh=Vh, vcol=vcol: e.activation(out=Vh[:, t, vcol], in_=ps[5][:, 0:64], func=AF.Copy), reads=[("ps", 5)], writes=[("V", hb)])
        nrow = 65 if hb == 0 else 128
        pairs = []
        for j in range(NG):
            nkb = 4 * j + 4
            for kb in range(nkb):
                pairs.append(mla_pair(K, S, ps, h, j, kb, pr_i % 2, QT, KT, Vh, Pb, OTs, rcp, nrow, kb == nkb - 1))
                pr_i += 1
        skew_emit(pairs)
    if EVEN_MODE == "mla":
        S.barrier()
        for c in range(4, 8):
            S.op("pool", lambda e, c=c: e.memset(K.mT[:, c, :], 0.0), writes=[("mT", c)])
        return
    stage_s5(K, l)


def rr_sin(S, y_ap, dst, shift, wk_f, wk_i, rd, wr):
    S.op("dve", lambda e: e.tensor_scalar(out=wk_f, in0=y_ap, scalar1=shift, scalar2=None, op0=ALU.add), reads=rd, writes=["rr_f"])
    S.op("dve", lambda e: e.tensor_copy(out=wk_i, in_=wk_f), reads=["rr_f"], writes=["rr_i"])
    S.op("dve", lambda e: e.tensor_copy(out=dst, in_=wk_i), reads=["rr_i"], writes=wr)
    S.op("dve", lambda e: e.tensor_tensor(out=dst, in0=wk_f, in1=dst, op=ALU.subtract), reads=["rr_f"] + wr, writes=wr)
    S.op("act", lambda e: e.activation(out=dst, in_=dst, func=AF.Sin, scale=TWO_PI), reads=wr, writes=wr)


def stage_s5(K, l):
    S, ps, dr, nc = K.S, K.ps, K.dr, K.nc
    S.barrier()
    A = K.A
    alA = Alloc(A, 0, 8192)
    CtH = alA.f32(2048); StH = alA.f32(2048)
    BP = alA.bf16(16 * 2 * 128).rearrange("p (g v n) -> p g v n", g=16, v=2)
    CP = alA.bf16(16 * 2 * 128).rearrange("p (g v n) -> p g v n", g=16, v=2)
    alB = Alloc(A, 16384)
    S5U = alB.bf16(4 * L).rearrange("p (c t) -> p c t", c=4)
    v3 = lambda ap: ap.rearrange("p (a b) -> p a b", a=4)
    alC = Alloc(A, 12288, 16384)
    lre = alC.f32(256); lim = alC.f32(256); ldt = alC.f32(256); bre = alC.f32(256); bim = alC.f32(256)
    er = alC.f32(256); cs_ = alC.f32(256); sn_ = alC.f32(256); den = alC.f32(256)
    wr_ = alC.f32(256); wi_ = alC.f32(256); ta = alC.f32(256); tb = alC.f32(256)
    Bfr = alB.f32(256); Bfi = alB.f32(256)
    wkf = alC.f32(256); wki = alC.i32(256)
    thp = alB.f32(96).rearrange("p (g v) -> p g v", v=3)
    th = alB.f32(32); rcol = alB.f32(32); crT = alB.f32(32); srT = alB.f32(32); dt2 = alB.f32(32); lr2 = alB.f32(32); y128 = alB.f32(32)
    carry = alB.f32(32); ctmp = alB.f32(4); ctmp2 = alB.f32(4)
    c1raw = alB.f32(512).rearrange("p (g c) -> p g c", g=32); c2raw = alB.f32(512).rearrange("p (g c) -> p g c", g=32)
    DSK = alB.bf16(4 * 128).rearrange("p (c n) -> p c n", c=4)
    dcol = alB.f32(4); bglu = alB.f32(4)
    WGLU = alB.bf16(4 * 512).rearrange("p (c n) -> p c n", c=4)
    bt = [alB.f32(512) for _ in range(2)]
    t1 = [alB.f32(512) for _ in range(2)]
    xt = [alB.f32(512) for _ in range(2)]
    z1 = [alB.bf16(512) for _ in range(2)]
    z2 = [alB.bf16(512) for _ in range(2)]
    CF = K.identf
    rowmask = K.rowmask
    sign1 = K.sign1
    for c in range(4):
        S.op("sp", lambda e, c=c: e.dma_start(out=S5U[:, c, :], in_=K.s5u_d[c]), reads=[("s5u_d", c)], writes=[("s5u", c, j) for j in range(NT)], dma=True)
    for nm, ap in (("s5_lam_re", lre), ("s5_lam_im", lim), ("s5_logdt", ldt), ("s5_b_re", bre), ("s5_b_im", bim)):
        S.op("sp", lambda e, nm=nm, ap=ap: e.dma_start(out=ap, in_=dr[nm].rearrange("p a b -> p (a b)")), writes=["s5prm"], dma=True)
    S.op("sp", lambda e: e.dma_start(out=thp, in_=dr["s5_thp"]), writes=["thp"], dma=True)
    S.op("sp", lambda e: e.dma_start(out=c1raw, in_=dr["s5_c1"]), writes=["craw"], dma=True)
    S.op("sp", lambda e: e.dma_start(out=c2raw, in_=dr["s5_c2"]), writes=["craw"], dma=True)
    S.op("sp", lambda e: e.dma_start(out=dcol, in_=dr["s5_d_col"]), writes=["dcol"], dma=True)
    S.op("sp", lambda e: e.dma_start(out=bglu, in_=dr["s5_bglu_col"]), writes=["bglu"], dma=True)
    S.op("pool", lambda e: e.dma_start(out=WGLU, in_=dr["s5_w_glu"].rearrange("(c p) n -> p c n", p=128)), writes=["WGLU"], dma=True)
    D_ = lambda fn, rd, wr: S.op("dve", fn, reads=rd, writes=wr)
    P = ["s5prm"]
    D_(lambda e: e.tensor_scalar(out=lre, in0=lre, scalar1=-1e-4, scalar2=None, op0=ALU.min), P, P)
    S.op("act", lambda e: e.activation(out=ldt, in_=ldt, func=AF.Exp), reads=P, writes=P)
    D_(lambda e: e.tensor_tensor(out=ta, in0=lre, in1=ldt, op=ALU.mult), P, ["ta"])
    S.op("act", lambda e: e.activation(out=er, in_=ta, func=AF.Exp), reads=["ta"], writes=["er"])
    D_(lambda e: e.tensor_tensor(out=tb, in0=lim, in1=ldt, op=ALU.mult), P, ["tb"])
    D_(lambda e: e.tensor_scalar(out=tb, in0=tb, scalar1=1.0 / (2 * math.pi), scalar2=None, op0=ALU.mult), ["tb"], ["tb"])
    rr_sin(S, tb, sn_, 0.0, wkf, wki, ["tb"], ["sn_"])
    rr_sin(S, tb, cs_, 0.25, wkf, wki, ["tb"], ["cs_"])
    D_(lambda e: e.tensor_tensor(out=cs_, in0=cs_, in1=er, op=ALU.mult), ["cs_", "er"], ["cs_"])
    D_(lambda e: e.tensor_scalar(out=cs_, in0=cs_, scalar1=-1.0, scalar2=None, op0=ALU.add), ["cs_"], ["cs_"])
    D_(lambda e: e.tensor_tensor(out=sn_, in0=sn_, in1=er, op=ALU.mult), ["sn_", "er"], ["sn_"])
    D_(lambda e: e.tensor_tensor(out=den, in0=lre, in1=lre, op=ALU.mult), P, ["den"])
    D_(lambda e: e.tensor_tensor(out=ta, in0=lim, in1=lim, op=ALU.mult), P + ["er"], ["ta"])
    D_(lambda e: e.tensor_tensor(out=den, in0=den, in1=ta, op=ALU.add), ["den", "ta"], ["den"])
    D_(lambda e: e.reciprocal(out=den, in_=den), ["den"], ["den"])
    D_(lambda e: e.tensor_tensor(out=ta, in0=cs_, in1=lre, op=ALU.mult), ["cs_", "den"] + P, ["ta"])
    D_(lambda e: e.tensor_tensor(out=tb, in0=sn_, in1=lim, op=ALU.mult), ["sn_", "cs_"] + P, ["tb"])
    D_(lambda e: e.tensor_tensor(out=ta, in0=ta, in1=tb, op=ALU.add), ["ta", "tb"], ["ta"])
    D_(lambda e: e.tensor_tensor(out=wr_, in0=ta, in1=den, op=ALU.mult), ["ta", "den"], ["wr_"])
    D_(lambda e: e.tensor_tensor(out=ta, in0=sn_, in1=lre, op=ALU.mult), ["sn_", "wr_"] + P, ["ta"])
    D_(lambda e: e.tensor_tensor(out=tb, in0=cs_, in1=lim, op=ALU.mult), ["cs_", "wr_"] + P, ["tb"])
    D_(lambda e: e.tensor_tensor(out=ta, in0=ta, in1=tb, op=ALU.subtract), ["ta", "tb"], ["ta"])
    D_(lambda e: e.tensor_tensor(out=wi_, in0=ta, in1=den, op=ALU.mult), ["ta", "den"], ["wi_"])
    D_(lambda e: e.tensor_tensor(out=ta, in0=wr_, in1=bre, op=ALU.mult), ["wr_", "wi_"] + P, ["ta"])
    D_(lambda e: e.tensor_tensor(out=tb, in0=wi_, in1=bim, op=ALU.mult), ["wi_"] + P, ["tb"])
    D_(lambda e: e.tensor_tensor(out=Bfr, in0=ta, in1=tb, op=ALU.subtract), ["ta", "tb"], ["Bf"])
    D_(lambda e: e.tensor_tensor(out=ta, in0=wr_, in1=bim, op=ALU.mult), ["wr_", "Bf"] + P, ["ta"])
    D_(lambda e: e.tensor_tensor(out=tb, in0=wi_, in1=bre, op=ALU.mult), ["wi_", "Bf"] + P, ["tb"])
    D_(lambda e: e.tensor_tensor(out=Bfi, in0=ta, in1=tb, op=ALU.add), ["ta", "tb"], ["Bf"])
    T = ["thp"]
    D_(lambda e: e.tensor_scalar(out=lr2, in0=thp[:, :, 0], scalar1=-1e-4, scalar2=None, op0=ALU.min), T, ["lr2"])
    S.op("act", lambda e: e.activation(out=dt2, in_=thp[:, :, 2], func=AF.Exp), reads=T, writes=["dt2"])
    D_(lambda e: e.tensor_tensor(out=lr2, in0=lr2, in1=dt2, op=ALU.mult), ["lr2", "dt2"], ["lr2"])
    S.op("act", lambda e: e.activation(out=rcol, in_=lr2, func=AF.Exp), reads=["lr2"], writes=["rcol"])
    D_(lambda e: e.tensor_tensor(out=th, in0=thp[:, :, 1], in1=dt2, op=ALU.mult), T + ["dt2"], ["th"])
    D_(lambda e: e.tensor_scalar(out=th, in0=th, scalar1=1.0 / (2 * math.pi), scalar2=None, op0=ALU.mult), ["th"], ["th"])
    D_(lambda e: e.tensor_scalar(out=y128, in0=th, scalar1=128.0, scalar2=None, op0=ALU.mult), ["th"], ["y128"])
    rr_sin(S, y128, srT, 0.0, wkf[:, 0:32], wki[:, 0:32], ["y128"], ["srT"])
    rr_sin(S, y128, crT, 0.25, wkf[:, 0:32], wki[:, 0:32], ["y128"], ["crT"])
    for c in range(4):
        D_(lambda e, c=c: e.tensor_scalar(out=DSK[:, c, :], in0=K.identb, scalar1=dcol[:, c:c + 1], scalar2=None, op0=ALU.mult), ["dcol", "CB"], ["DSK"])
    Bfr3, Bfi3 = v3(Bfr), v3(Bfi)
    CtH3 = CtH.rearrange("p (g t) -> p g t", g=16); StH3 = StH.rearrange("p (g t) -> p g t", g=16)
    def s5_unit(hf, j, quad, b):
        tsl = slice(128 * j, 128 * j + 128)
        c = 2 * hf + quad // 2
        pA, pB = (0, 1) if b == 0 else (2, 3)
        pY = 5 + (j * 2 + quad // 2) % 2
        qsl = slice(512 * quad, 512 * quad + 512)
        first = (quad % 2 == 0)

        def A_():
            for i in range(4):
                gq = 4 * quad + i
                S.op("pe", lambda e, gq=gq, i=i: e.matmul(ps[pA][:, 128 * i:128 * i + 128], lhsT=BP[:, gq, 0, :], rhs=S5U[:, c, tsl], start=True, stop=True),
                     reads=["BP", ("s5u", c, j)], writes=[("ps", pA)])
                S.op("pe", lambda e, gq=gq, i=i: e.matmul(ps[pB][:, 128 * i:128 * i + 128], lhsT=BP[:, gq, 1, :], rhs=S5U[:, c, tsl], start=True, stop=True),
                     reads=["BP", ("s5u", c, j)], writes=[("ps", pB)])
            D_(lambda e: e.tensor_tensor(out=t1[b], in0=ps[pA][:, :], in1=CtH[:, qsl], op=ALU.mult), [("ps", pA), "tabs"], [("t1", b)])
            D_(lambda e: e.tensor_tensor(out=bt[b], in0=ps[pB][:, :], in1=StH[:, qsl], op=ALU.mult), [("ps", pB), "tabs"], [("bt", b)])
            S.op("pool", lambda e: e.tensor_tensor(out=bt[b], in0=bt[b], in1=t1[b], op=ALU.add), reads=[("bt", b), ("t1", b)], writes=[("bt", b)])

        def B_():
            for i in range(4):
                g = 16 * hf + 4 * quad + i
                init = 0.0 if j == 0 else carry[:, g:g + 1]
                D_(lambda e, i=i, g=g, init=init: e.tensor_tensor_scan(out=xt[b][:, 128 * i:128 * i + 128], data0=rcol[:, g:g + 1].to_broadcast([128, 128]),
                                                                      data1=bt[b][:, 128 * i:128 * i + 128], initial=init, op0=ALU.mult, op1=ALU.add),
                   [("bt", b), "rcol", ("carry", hf, quad)], [("xt", b)])
            if j < NT - 1:
                g0 = 16 * hf + 4 * quad
                xl = xt[b].rearrange("p (i t) -> p i t", i=4)[:, :, 127]
                S.op("pe", lambda e: e.matmul(ps[4][:, 0:4], lhsT=K.swapf, rhs=xl, start=True, stop=True), reads=[("xt", b), "CF"], writes=[("ps", 4)])
                D_(lambda e: e.tensor_tensor(out=ctmp, in0=xl, in1=crT[:, g0:g0 + 4], op=ALU.mult), [("xt", b), "crT"], ["ctmp"])
                D_(lambda e: e.tensor_tensor(out=ctmp2, in0=ps[4][:, 0:4], in1=srT[:, g0:g0 + 4], op=ALU.mult), [("ps", 4), "srT"], ["ctmp2"])
                D_(lambda e: e.tensor_tensor(out=carry[:, g0:g0 + 4], in0=ctmp, in1=ctmp2, op=ALU.add), ["ctmp", "ctmp2"], [("carry", hf, quad)])
            S.op("pool", lambda e: e.tensor_tensor(out=z1[b], in0=xt[b], in1=CtH[:, qsl], op=ALU.mult), reads=[("xt", b), "tabs"], writes=[("z1", b)])
            D_(lambda e: e.tensor_tensor(out=z2[b], in0=xt[b], in1=StH[:, qsl], op=ALU.mult), [("xt", b), "tabs"], [("z2", b)])
            if first:
                S.op("pe", lambda e: e.matmul(ps[pY][:, 0:128], lhsT=DSK[:, c, :], rhs=S5U[:, c, tsl], start=True, stop=False),
                     reads=["DSK", ("s5u", c, j)], writes=[("ps", pY)])
            for i in range(4):
                gq = 4 * quad + i
                last = (not first) and i == 3
                S.op("pe", lambda e, gq=gq, i=i: e.matmul(ps[pY][:, 0:128], lhsT=CP[:, gq, 0, :], rhs=z1[b][:, 128 * i:128 * i + 128], start=False, stop=False),
                     reads=["CP", ("z1", b)], writes=[("ps", pY)])
                S.op("pe", lambda e, gq=gq, i=i, last=last: e.matmul(ps[pY][:, 0:128], lhsT=CP[:, gq, 1, :], rhs=z2[b][:, 128 * i:128 * i + 128], start=False, stop=last),
                     reads=["CP", ("z2", b)], writes=[("ps", pY)])
            if not first:
                S.op("act", lambda e: e.activation(out=S5U[:, c, tsl], in_=ps[pY][:, 0:128], func=GELU_TANH), reads=[("ps", pY)], writes=[("s5u", c, j)])
        return A_, B_

    unit = 0
    for hf in range(2):
        for gq in range(16):
            g = 16 * hf + gq
            gh, gl = g // 8, g % 8
            rm = rowmask[:, gl:gl + 1]
            D_(lambda e, gq=gq, gh=gh, rm=rm: e.tensor_scalar(out=BP[:, gq, 0, 0:64], in0=Bfr3[:, gh, :], scalar1=rm, scalar2=None, op0=ALU.mult), ["Bf", "CF"], ["BP"])
            D_(lambda e, gq=gq, gh=gh, rm=rm: e.tensor_scalar(out=BP[:, gq, 0, 64:128], in0=Bfi3[:, gh, :], scalar1=rm, scalar2=None, op0=ALU.mult), ["Bf", "CF"], ["BP"])
            D_(lambda e, gq=gq, gh=gh, rm=rm: e.tensor_scalar(out=BP[:, gq, 1, 0:64], in0=Bfi3[:, gh, :], scalar1=rm, scalar2=None, op0=ALU.mult), ["Bf", "CF"], ["BP"])
            D_(lambda e, gq=gq, gh=gh, rm=rm: e.tensor_scalar(out=BP[:, gq, 1, 64:128], in0=Bfr3[:, gh, :], scalar1=rm, scalar2=-1.0, op0=ALU.mult, op1=ALU.mult), ["Bf", "CF"], ["BP"])
        S.op("pool", lambda e: e.memset(CP.rearrange("p g v n -> p (g v n)"), 0.0), writes=["CP"])
        for gq in range(16):
            g = 16 * hf + gq
            gl = g % 8
            D_(lambda e, gq=gq, g=g, gl=gl: e.tensor_scalar(out=CP[:, gq, 0, 16 * gl:16 * gl + 16], in0=c1raw[:, g, :], scalar1=sign1, scalar2=None, op0=ALU.mult), ["craw", "CF"], ["CP"])
            D_(lambda e, gq=gq, g=g, gl=gl: e.tensor_scalar(out=CP[:, gq, 1, 16 * gl:16 * gl + 16], in0=c2raw[:, g, :], scalar1=-1.0, scalar2=None, op0=ALU.mult), ["craw"], ["CP"])
            D_(lambda e, gq=gq, g=g: e.tensor_scalar(out=t1[0][:, 0:128], in0=K.iota, scalar1=th[:, g:g + 1], scalar2=None, op0=ALU.mult), ["th", "CF"], ["yturn"])
            rr_sin(S, t1[0][:, 0:128], StH3[:, gq, :], 0.0, wkf[:, 0:128], wki[:, 0:128], ["yturn"], ["tabs"])
            rr_sin(S, t1[0][:, 0:128], CtH3[:, gq, :], 0.25, wkf[:, 0:128], wki[:, 0:128], ["yturn"], ["tabs"])
        units = []
        for j in range(NT):
            for quad in range(4):
                units.append(s5_unit(hf, j, quad, unit % 2))
                unit += 1
        skew_emit(units)
    sg = bt
    for tg in range(NG):
        ts_ = slice(512 * tg, 512 * tg + 512)
        for oc in range(4):
            pi = (tg * 4 + oc) % 4
            for kc in range(4):
                S.op("pe", lambda e, pi=pi, kc=kc, oc=oc, ts_=ts_: e.matmul(ps[pi][:, :], lhsT=WGLU[:, kc, 128 * oc:128 * oc + 128], rhs=S5U[:, kc, ts_], start=(kc == 0), stop=(kc == 3)),
                     reads=["WGLU"] + [("s5u", kc, jj) for jj in range(4 * tg, 4 * tg + 4)], writes=[("ps", pi)])
            sb_ = sg[oc % 2]
            S.op("act", lambda e, pi=pi, sb_=sb_, oc=oc: e.activation(out=sb_, in_=ps[pi][:, :], func=AF.Sigmoid, bias=bglu[:, oc:oc + 1]), reads=[("ps", pi), "bglu"], writes=[("bt", oc % 2)])
            D_(lambda e, sb_=sb_, oc=oc, ts_=ts_: e.tensor_tensor(out=K.mT[:, 4 + oc, ts_], in0=S5U[:, oc, ts_], in1=sb_, op=ALU.mult),
               [("bt", oc % 2)] + [("s5u", oc, jj) for jj in range(4 * tg, 4 * tg + 4)], [("mT", 4 + oc)])
    if EVEN_MODE == "s5":
        S.barrier()
        for c in range(4):
            S.op("pool", lambda e, c=c: e.memset(K.mT[:, c, :], 0.0), writes=[("mT", c)])
```
